# Optimizing a Trainium2 kernel written in Bass

```python
import math
import jax
import jax.numpy as jnp
from jax import lax
import numpy as np

D_MODEL = 2048
BATCH = 2
SEQ = 4096
DEPTH = 1

CTX_LEN = 256
GRID_W = 64

D_S5 = 1024
S5_GROUP = 16
S5_GROUPS = D_S5 // S5_GROUP
S5_STATE = 64
S5_DIRS = 2
LAMBDA_RE_MAX = -1e-4

D_HY = 1024
HY_ORDER = 2
HY_DIRS = 2
HY_SHORT = 3
HY_EMB = 33
HY_BANDS = (HY_EMB - 1) // 2
HY_FF = 64
HY_TARGET = 1e-2
HY_FAST_PCT = 0.3
HY_SLOW_PCT = 1.5

N_BRANCH = 2
I_HY = D_S5
I_GA = D_S5 + (HY_ORDER + 1) * D_HY
I_GB = I_GA + D_MODEL
D_IN = I_GB + D_MODEL
D_FF = 5632
N_SUB = 3
N_MOD = 3
HALF_STEP = 0.5
RMS_EPS = 1e-6

kernel_name = "hybrid_s5_hyena_macaron_dit_layer"


def _rmsnorm(x, g):
    xf = x.astype(jnp.float32)
    y = xf * lax.rsqrt(jnp.mean(xf * xf, axis=-1, keepdims=True) + RMS_EPS)
    return (y * g.astype(jnp.float32)).astype(x.dtype)


def _modulate(h, shift, scale):
    return h * (1.0 + scale) + shift


def _swiglu(h, w_gate, w_up, w_down):
    return (jax.nn.silu(h @ w_gate) * (h @ w_up)) @ w_down


def _ffn_sublayer(x, m, gain, w_gate, w_up, w_down):
    h = _modulate(_rmsnorm(x, gain), m[:, 0], m[:, 1])
    return x + HALF_STEP * m[:, 2] * _swiglu(h, w_gate, w_up, w_down)


def _s5_discretise(lam_re, lam_im, log_dt, b_re, b_im):
    f32 = jnp.float32
    dt = jnp.exp(log_dt.astype(f32))[:, None]
    lr = jnp.minimum(lam_re.astype(f32), LAMBDA_RE_MAX)
    li = lam_im.astype(f32)
    mag = jnp.exp(lr * dt)
    ab_re = mag * jnp.cos(li * dt)
    ab_im = mag * jnp.sin(li * dt)
    nr, ni = ab_re - 1.0, ab_im
    den = lr * lr + li * li
    f_re = (nr * lr + ni * li) / den
    f_im = (ni * lr - nr * li) / den
    br, bi = b_re.astype(f32), b_im.astype(f32)
    bb_re = f_re[..., None] * br - f_im[..., None] * bi
    bb_im = f_re[..., None] * bi + f_im[..., None] * br
    return ab_re, ab_im, bb_re, bb_im


def _affine_combine(e1, e2):
    a1r, a1i, b1r, b1i = e1
    a2r, a2i, b2r, b2i = e2
    return (a2r * a1r - a2i * a1i,
            a2r * a1i + a2i * a1r,
            a2r * b1r - a2i * b1i + b2r,
            a2r * b1i + a2i * b1r + b2i)


def _s5_states(u, disc, h0, reverse):
    ab_re, ab_im, bb_re, bb_im = disc
    bu_re = jnp.einsum('blgh,gph->blgp', u, bb_re)
    bu_im = jnp.einsum('blgh,gph->blgp', u, bb_im)
    if h0 is not None:
        first = -1 if reverse else 0
        h_re, h_im = h0
        bu_re = bu_re.at[:, first].add(ab_re * h_re - ab_im * h_im)
        bu_im = bu_im.at[:, first].add(ab_re * h_im + ab_im * h_re)
    a_re = jnp.broadcast_to(ab_re, bu_re.shape)
    a_im = jnp.broadcast_to(ab_im, bu_im.shape)
    _, _, s_re, s_im = lax.associative_scan(_affine_combine, (a_re, a_im, bu_re, bu_im),
                                            reverse=reverse, axis=1)
    return s_re, s_im


def _s5_readout(s_re, s_im, c_re, c_im):
    return jnp.einsum('blgp,ghp->blgh', s_re, c_re) - jnp.einsum('blgp,ghp->blgh', s_im, c_im)


def _s5_bidirectional(u, u_ctx, lam_re, lam_im, log_dt, b_re, b_im, c_re, c_im, d_skip, ctx_out):
    f32 = jnp.float32
    bsz, length, _ = u.shape
    ctx_len = u_ctx.shape[1]
    ug = u.astype(f32).reshape(bsz, length, S5_GROUPS, S5_GROUP)
    ucg = u_ctx.astype(f32).reshape(bsz, ctx_len, S5_GROUPS, S5_GROUP)
    dg = d_skip.astype(f32).reshape(S5_GROUPS, S5_GROUP)
    y = ug * dg
    y_ctx = ucg * dg if ctx_out else None
    for direction, rev in enumerate((False, True)):
        disc = _s5_discretise(lam_re[direction], lam_im[direction], log_dt[direction],
                              b_re[direction], b_im[direction])
        cr = c_re[direction].astype(f32)
        ci = c_im[direction].astype(f32)
        sc_re, sc_im = _s5_states(ucg, disc, None, rev)
        last = 0 if rev else -1
        h_ctx = (sc_re[:, last], sc_im[:, last])
        sl_re, sl_im = _s5_states(ug, disc, h_ctx, rev)
        y = y + _s5_readout(sl_re, sl_im, cr, ci)
        if ctx_out:
            y_ctx = y_ctx + _s5_readout(sc_re, sc_im, cr, ci)
    y = y.reshape(bsz, length, D_S5)
    if ctx_out:
        y_ctx = y_ctx.reshape(bsz, ctx_len, D_S5)
    return y, y_ctx


def _short_conv_rows(u, w, b, n_rows, row_len):
    bsz, length, ch = u.shape
    ug = u.reshape(bsz, n_rows, row_len, ch)
    pad = HY_SHORT // 2
    up = jnp.pad(ug, ((0, 0), (0, 0), (pad, pad), (0, 0)))
    y = b + up[:, :, 0:row_len] * w[0]
    for j in range(1, HY_SHORT):
        y = y + up[:, :, j:j + row_len] * w[j]
    return y.reshape(bsz, length, ch)


def _hyena_filter_spectrum(length, w1, b1, w2, b2, w3, b3, freq, w_out):
    f32 = jnp.float32
    t = jnp.linspace(0.0, 1.0, length, dtype=f32)[:, None]
    w = (2.0 * math.pi / length) * jnp.arange(length, dtype=f32)[:, None]
    bands = jnp.linspace(1e-4, HY_BANDS - 1, HY_BANDS, dtype=f32)[None, :]
    feats = jnp.concatenate([t, jnp.cos(bands * w), -jnp.sin(bands * w)], axis=-1)
    fr = freq.astype(f32)
    hdn = jnp.sin(fr * (feats @ w1.astype(f32) + b1.astype(f32)))
    hdn = jnp.sin(fr * (hdn @ w2.astype(f32) + b2.astype(f32)))
    hdn = jnp.sin(fr * (hdn @ w3.astype(f32) + b3.astype(f32)))
    k = hdn @ w_out.astype(f32)
    n_ch = k.shape[-1]
    deltas = jnp.abs(jnp.linspace(math.log(HY_TARGET) / HY_SLOW_PCT,
                                  math.log(HY_TARGET) / HY_FAST_PCT, n_ch, dtype=f32))
    k = (k * jnp.exp(-t * deltas)).reshape(length, HY_ORDER, HY_DIRS, D_HY)
    k_two = jnp.concatenate([k[:, :, 0],
                             jnp.zeros((1, HY_ORDER, D_HY), f32),
                             k[:0:-1, :, 1]], axis=0)
    k_two = k_two / jnp.sum(jnp.abs(k_two), axis=0, keepdims=True)
    return jnp.fft.rfft(k_two, axis=0)


def _hyena(u, n_rows, row_len, short_w, short_b, w1, b1, w2, b2, w3, b3, freq, w_out, bias):
    f32 = jnp.float32
    length = u.shape[1]
    us = _short_conv_rows(u, short_w, short_b, n_rows, row_len).astype(f32)
    parts = jnp.split(us, HY_ORDER + 1, axis=-1)
    z, gates = parts[0], parts[1:]
    k_f = _hyena_filter_spectrum(length, w1, b1, w2, b2, w3, b3, freq, w_out)
    bias = bias.astype(f32)
    n_fft = 2 * length
    for o in range(HY_ORDER):
        conv = jnp.fft.irfft(jnp.fft.rfft(z, n=n_fft, axis=1) * k_f[None, :, o], n=n_fft, axis=1)[:, :length]
        z = gates[o] * (conv + bias[o] * z)
    return z


def _merge_branches(y_s5, y_hy, g_a, g_b, w_pa, w_pb, w_out):
    s = jax.nn.gelu(y_s5)
    pa = s @ w_pa
    y_a = pa[..., :D_MODEL] * jax.nn.sigmoid(pa[..., D_MODEL:])
    y_b = y_hy @ w_pb
    return (jax.nn.sigmoid(g_a) * y_a + jax.nn.sigmoid(g_b) * y_b) @ w_out


def setup_inputs(seed: int = 0) -> dict:
    key = jax.random.key(seed)
    keys = iter(jax.random.split(key, 48))
    f32 = jnp.float32

    def normal(shape, scale):
        return scale * jax.random.normal(next(keys), shape, f32)

    x = normal((BATCH, SEQ, D_MODEL), 1.0)
    c = normal((BATCH, D_MODEL), 1.0)
    ctx = normal((BATCH, CTX_LEN, D_MODEL), 1.0)
    c_ctx = normal((D_MODEL,), 1.0)
    w_ada = normal((DEPTH, D_MODEL, N_SUB * N_MOD * D_MODEL), D_MODEL ** -0.5)
    b_ada = normal((DEPTH, N_SUB * N_MOD * D_MODEL), 0.02)
    norm_g = 1.0 + normal((DEPTH, N_SUB, D_MODEL), 0.01)
    ffn_w_gate = normal((DEPTH, 2, D_MODEL, D_FF), D_MODEL ** -0.5)
    ffn_w_up = normal((DEPTH, 2, D_MODEL, D_FF), D_MODEL ** -0.5)
    ffn_w_down = normal((DEPTH, 2, D_FF, D_MODEL), D_FF ** -0.5)
    w_in = normal((DEPTH, D_MODEL, D_IN), D_MODEL ** -0.5)
    s5_shape = (DEPTH, S5_DIRS, S5_GROUPS, S5_STATE)
    s5_lam_re = -0.5 + normal(s5_shape, 0.02)
    s5_lam_im = math.pi * jnp.arange(S5_STATE, dtype=f32) + normal(s5_shape, 0.02)
    s5_log_dt = jax.random.uniform(next(keys), (DEPTH, S5_DIRS, S5_GROUPS), f32,
                                   math.log(1e-3), math.log(1e-1))
    b_shape = (DEPTH, S5_DIRS, S5_GROUPS, S5_STATE, S5_GROUP)
    s5_b_re = normal(b_shape, (2.0 * S5_GROUP) ** -0.5)
    s5_b_im = normal(b_shape, (2.0 * S5_GROUP) ** -0.5)
    c_shape = (DEPTH, S5_DIRS, S5_GROUPS, S5_GROUP, S5_STATE)
    s5_c_re = normal(c_shape, (2.0 * S5_STATE) ** -0.5)
    s5_c_im = normal(c_shape, (2.0 * S5_STATE) ** -0.5)
    s5_d = normal((DEPTH, D_S5), 1.0)
    hy_short_w = normal((DEPTH, HY_SHORT, (HY_ORDER + 1) * D_HY), HY_SHORT ** -0.5)
    hy_short_b = normal((DEPTH, (HY_ORDER + 1) * D_HY), 0.02)
    hy_w1 = normal((DEPTH, HY_EMB, HY_FF), HY_EMB ** -0.5)
    hy_b1 = normal((DEPTH, HY_FF), 0.1)
    hy_w2 = normal((DEPTH, HY_FF, HY_FF), HY_FF ** -0.5)
    hy_b2 = normal((DEPTH, HY_FF), 0.1)
    hy_w3 = normal((DEPTH, HY_FF, HY_FF), HY_FF ** -0.5)
    hy_b3 = normal((DEPTH, HY_FF), 0.1)
    hy_freq = 1.0 + normal((DEPTH, HY_FF), 0.1)
    hy_w_out = normal((DEPTH, HY_FF, HY_ORDER * HY_DIRS * D_HY), HY_FF ** -0.5)
    hy_bias = normal((DEPTH, HY_ORDER, D_HY), 0.5)
    w_pa = normal((DEPTH, D_S5, 2 * D_MODEL), D_S5 ** -0.5)
    w_pb = normal((DEPTH, D_HY, D_MODEL), D_HY ** -0.5)
    w_out = normal((DEPTH, D_MODEL, D_MODEL), D_MODEL ** -0.5)
    final_g = 1.0 + normal((D_MODEL,), 0.01)
    return {"x": x, "c": c, "ctx": ctx, "c_ctx": c_ctx,
            "w_ada": w_ada, "b_ada": b_ada, "norm_g": norm_g,
            "ffn_w_gate": ffn_w_gate, "ffn_w_up": ffn_w_up, "ffn_w_down": ffn_w_down,
            "w_in": w_in,
            "s5_lam_re": s5_lam_re, "s5_lam_im": s5_lam_im, "s5_log_dt": s5_log_dt,
            "s5_b_re": s5_b_re, "s5_b_im": s5_b_im, "s5_c_re": s5_c_re, "s5_c_im": s5_c_im,
            "s5_d": s5_d,
            "hy_short_w": hy_short_w, "hy_short_b": hy_short_b,
            "hy_w1": hy_w1, "hy_b1": hy_b1, "hy_w2": hy_w2, "hy_b2": hy_b2,
            "hy_w3": hy_w3, "hy_b3": hy_b3, "hy_freq": hy_freq, "hy_w_out": hy_w_out,
            "hy_bias": hy_bias,
            "w_pa": w_pa, "w_pb": w_pb, "w_out": w_out, "final_g": final_g}


def reference(x, c, ctx, c_ctx, w_ada, b_ada, norm_g, ffn_w_gate, ffn_w_up, ffn_w_down, w_in,
              s5_lam_re, s5_lam_im, s5_log_dt, s5_b_re, s5_b_im, s5_c_re, s5_c_im, s5_d,
              hy_short_w, hy_short_b, hy_w1, hy_b1, hy_w2, hy_b2, hy_w3, hy_b3, hy_freq, hy_w_out,
              hy_bias, w_pa, w_pb, w_out, final_g):
    bsz = x.shape[0]
    n_rows = x.shape[1] // GRID_W
    ctx_len = ctx.shape[1]
    for l in range(DEPTH):
        update_ctx = l < DEPTH - 1
        mod = (jax.nn.silu(c) @ w_ada[l] + b_ada[l]).reshape(bsz, N_SUB, N_MOD, 1, D_MODEL)
        mod_c = (jax.nn.silu(c_ctx) @ w_ada[l] + b_ada[l]).reshape(1, N_SUB, N_MOD, 1, D_MODEL)

        x = _ffn_sublayer(x, mod[:, 0], norm_g[l, 0], ffn_w_gate[l, 0], ffn_w_up[l, 0], ffn_w_down[l, 0])
        ctx = _ffn_sublayer(ctx, mod_c[:, 0], norm_g[l, 0], ffn_w_gate[l, 0], ffn_w_up[l, 0], ffn_w_down[l, 0])

        h = _modulate(_rmsnorm(x, norm_g[l, 1]), mod[:, 1, 0], mod[:, 1, 1])
        hc = _modulate(_rmsnorm(ctx, norm_g[l, 1]), mod_c[:, 1, 0], mod_c[:, 1, 1])
        proj = h @ w_in[l]
        proj_c = hc @ (w_in[l] if update_ctx else w_in[l][:, :D_S5])
        y_s5, y_s5_c = _s5_bidirectional(proj[..., :I_HY], proj_c[..., :D_S5],
                                         s5_lam_re[l], s5_lam_im[l], s5_log_dt[l],
                                         s5_b_re[l], s5_b_im[l], s5_c_re[l], s5_c_im[l], s5_d[l],
                                         update_ctx)
        y_hy = _hyena(proj[..., I_HY:I_GA], n_rows, GRID_W, hy_short_w[l], hy_short_b[l],
                      hy_w1[l], hy_b1[l], hy_w2[l], hy_b2[l], hy_w3[l], hy_b3[l], hy_freq[l],
                      hy_w_out[l], hy_bias[l])
        mixed = _merge_branches(y_s5.astype(x.dtype), y_hy.astype(x.dtype),
                                proj[..., I_GA:I_GB], proj[..., I_GB:], w_pa[l], w_pb[l], w_out[l])
        x = x + mod[:, 1, 2] * mixed
        if update_ctx:
            y_hy_c = _hyena(proj_c[..., I_HY:I_GA], 1, ctx_len, hy_short_w[l], hy_short_b[l],
                            hy_w1[l], hy_b1[l], hy_w2[l], hy_b2[l], hy_w3[l], hy_b3[l], hy_freq[l],
                            hy_w_out[l], hy_bias[l])
            mixed_c = _merge_branches(y_s5_c.astype(ctx.dtype), y_hy_c.astype(ctx.dtype),
                                      proj_c[..., I_GA:I_GB], proj_c[..., I_GB:],
                                      w_pa[l], w_pb[l], w_out[l])
            ctx = ctx + mod_c[:, 1, 2] * mixed_c
            ctx = _ffn_sublayer(ctx, mod_c[:, 2], norm_g[l, 2], ffn_w_gate[l, 1], ffn_w_up[l, 1],
                                ffn_w_down[l, 1])

        x = _ffn_sublayer(x, mod[:, 2], norm_g[l, 2], ffn_w_gate[l, 1], ffn_w_up[l, 1], ffn_w_down[l, 1])
    return _rmsnorm(x, final_g)
```

```python
import numpy as np
from contextlib import ExitStack
import ml_dtypes
import concourse.bass as bass
import concourse.mybir as mybir
from concourse.bass_utils import run_bass_kernel_spmd

F32 = mybir.dt.float32
BF16 = mybir.dt.bfloat16
I32 = mybir.dt.int32
ALU = mybir.AluOpType
AF = mybir.ActivationFunctionType
AX = mybir.AxisListType

NCORES = 8
D = 2048
KT = 16
DFF = 5632
FT = 44
TX = 1024
TC = 64
T = TX + TC
CH = [(0, 512), (512, 1024), (1024, 1088)]
EPS = 1e-6


class Prog:
    ENG = ("pe", "act", "dve", "pool", "sp")

    def __init__(self, nc, ctx):
        self.nc = nc
        self.ctx = ctx
        self.q = {e: [] for e in self.ENG}
        self.esem = {e: ctx.enter_context(nc.semaphore("es_" + e)) for e in ("pe", "act", "dve", "pool")}
        self.ecnt = {e: 0 for e in self.esem}
        self.dsem = {}
        self.dcnt = {}
        self.res = {}
        self.seen = {e: {} for e in self.ENG}
        self.mute = False

    def _st(self, key):
        s = self.res.get(key)
        if s is None:
            s = {"w": None, "r": {}}
            self.res[key] = s
        return s

    def _need(self, reads, writes):
        need = {}

        def add(tok):
            if tok is None:
                return
            sem, val, name = tok
            if name not in need or need[name][1] < val:
                need[name] = (sem, val)

        for k in reads:
            add(self._st(k)["w"])
        for k in writes:
            s = self._st(k)
            add(s["w"])
            for name, (sem, val) in s["r"].items():
                add((sem, val, name))
        return need

    def _emit_waits(self, eng, need):
        seen = self.seen[eng]
        for name, (sem, val) in need.items():
            if eng == "pe" and name == "pe":
                continue
            if seen.get(name, 0) >= val:
                continue
            seen[name] = val
            self.q[eng].append(lambda e, sem=sem, val=val: e.wait_ge(sem, val))

    def _update(self, tok, reads, writes):
        sem, val, name = tok
        for k in reads:
            self._st(k)["r"][name] = (sem, val)
        for k in writes:
            s = self._st(k)
            s["w"] = tok
            s["r"] = {}

    def op(self, eng, fns, reads=(), writes=()):
        if self.mute:
            return
        if not isinstance(fns, (list, tuple)):
            fns = [fns]
        need = self._need(reads, writes)
        self._emit_waits(eng, need)
        sem = self.esem[eng]
        self.ecnt[eng] += 1
        val = self.ecnt[eng]
        for f in fns[:-1]:
            self.q[eng].append(lambda e, f=f: f(e))
        last = fns[-1]
        self.q[eng].append(lambda e, f=last, sem=sem: f(e).then_inc(sem, 1))
        self._update((sem, val, eng), reads, writes)

    def _dsem(self, skey):
        if skey not in self.dsem:
            self.dsem[skey] = self.ctx.enter_context(self.nc.semaphore("ds%d" % len(self.dsem)))
            self.dcnt[skey] = 0
        return self.dsem[skey]

    def dma(self, q, out, in_, reads=(), writes=(), skey=None, **kw):
        if self.mute:
            return
        skey = skey if skey is not None else writes[0]
        need = self._need(reads, writes)
        self._emit_waits(q, need)
        sem = self._dsem(skey)
        self.dcnt[skey] += 16
        val = self.dcnt[skey]
        self.q[q].append(lambda e, out=out, in_=in_, sem=sem, kw=kw: e.dma_start(out=out, in_=in_, **kw).then_inc(sem, 16))
        self._update((sem, val, "d:" + str(skey)), reads, writes)

    def barrier(self):
        if self.mute:
            return
        toks = [(self.esem[e], self.ecnt[e], e) for e in self.esem if self.ecnt[e] > 0]
        toks += [(self.dsem[k], self.dcnt[k], "d:" + str(k)) for k in self.dsem if self.dcnt[k] > 0]
        for eng in self.ENG:
            need = {name: (sem, val) for (sem, val, name) in toks}
            seen = self.seen[eng]
            for name, (sem, val) in need.items():
                if seen.get(name, 0) >= val:
                    continue
                seen[name] = val
                self.q[eng].append(lambda e, sem=sem, val=val: e.wait_ge(sem, val))

    def wait_all(self, q, keys):
        if self.mute:
            return
        self._emit_waits(q, self._need(keys, ()))

    def replay(self):
        with self.nc.Block() as block:
            @block.sync
            def _(e):
                for f in self.q["sp"]:
                    f(e)

            @block.tensor
            def _(e):
                for f in self.q["pe"]:
                    f(e)

            @block.scalar
            def _(e):
                for f in self.q["act"]:
                    f(e)

            @block.vector
            def _(e):
                for f in self.q["dve"]:
                    f(e)

            @block.gpsimd
            def _(e):
                for f in self.q["pool"]:
                    f(e)


ARENA_WORDS = 52600


class Ctx:
    def __init__(self, nc, ctx):
        self.nc, self.ctx = nc, ctx
        self.P = Prog(nc, ctx)
        self.banks = [ctx.enter_context(nc.psum_tensor("bank%d" % i, [128, 512], F32)) for i in range(8)]
        self.bi = 0
        self.arena = ctx.enter_context(nc.sbuf_tensor("arena", [128, ARENA_WORDS], F32))
        self.pptr = 0
        self.ptr = 0
        self.base = None
        self.peak = 0

    def sb(self, name, shape, dt=F32, persist=False):
        esz = {F32: 4, I32: 4, BF16: 2}[dt]
        n = int(np.prod(shape[1:]))
        words = (n * esz + 3) // 4
        if persist or self.base is None:
            assert self.base is None, "persistent allocations must precede phases"
            off = self.pptr
            self.pptr += words
        else:
            off = self.base + self.ptr
            self.ptr += words
        self.peak = max(self.peak, off + words)
        assert off + words <= ARENA_WORDS, ("SBUF arena overflow", name, off + words)
        ap = self.arena[:, off:off + words]
        if dt != F32:
            ap = ap.bitcast(dt)
        if esz == 2 and n % 2:
            ap = ap[:, 0:n]
        if len(shape) == 3:
            ap = ap.rearrange("p (a b) -> p a b", b=shape[2])
        elif len(shape) == 4:
            ap = ap.rearrange("p (a b c) -> p a b c", b=shape[2], c=shape[3])
        elif len(shape) == 5:
            ap = ap.rearrange("p (a b c d) -> p a b c d", b=shape[2], c=shape[3], d=shape[4])
        if shape[0] < 128:
            ap = ap[0:shape[0]]
        return ap

    def phase(self):
        if self.base is None:
            self.base = self.pptr
        self.P.barrier()
        self.ptr = 0

    def bank(self):
        b = self.bi
        self.bi = (self.bi + 1) % 8
        return b, self.banks[b]


def emit_norm_mod(C, xT, hT, sqb, rstd, onesb, gp, sh, tag, out32=None, okey="yT"):
    P = C.P
    sqb = hT
    for kt in range(KT):
        P.op("act", lambda e, kt=kt: e.activation(out=sqb[:, kt, :], in_=xT[:, kt, :], func=AF.Square),
             reads=[("xT", kt)], writes=[("hT", kt)])
    for ci, (a, b) in enumerate(CH):
        bi, bk = C.bank()
        fns = [lambda e, kt=kt, a=a, b=b, bk=bk: e.matmul(bk[:, 0:b - a], lhsT=onesb[:, :], rhs=sqb[:, kt, a:b], start=(kt == 0), stop=(kt == KT - 1))
               for kt in range(KT)]
        P.op("pe", fns, reads=[("hT", kt) for kt in range(KT)] + ["ones"], writes=[("bank", bi)])
        P.op("act", lambda e, a=a, b=b, bk=bk: e.activation(out=rstd[:, a:b], in_=bk[:, 0:b - a], func=AF.Sqrt, scale=1.0 / D, bias=C.epsb[:, 0:1]),
             reads=[("bank", bi), "epsb"], writes=[("rstd", ci)])
        P.op("dve", lambda e, a=a, b=b: e.reciprocal(out=rstd[:, a:b], in_=rstd[:, a:b]), reads=[("rstd", ci)], writes=[("rstd", ci)])
    for kt in range(KT):
        tmp = C.tmpn[kt % 2]
        P.op("dve", lambda e, kt=kt, tmp=tmp: e.tensor_tensor(out=tmp[:, :], in0=xT[:, kt, :], in1=rstd[:, :], op=ALU.mult),
             reads=[("xT", kt)] + [("rstd", ci) for ci in range(3)], writes=[("tmpn", kt % 2)])
        if out32 is not None:
            P.op("act", lambda e, kt=kt, tmp=tmp: e.activation(out=out32[:, kt, 0:TX], in_=tmp[:, 0:TX], func=AF.Identity, scale=gp[:, kt, 0:1], bias=sh[:, kt, 0:1]),
                 reads=[("tmpn", kt % 2), tag + "gp", tag + "sh"], writes=[(okey, kt)])
            continue
        P.op("act", [lambda e, kt=kt, tmp=tmp: e.activation(out=hT[:, kt, 0:TX], in_=tmp[:, 0:TX], func=AF.Identity, scale=gp[:, kt, 0:1], bias=sh[:, kt, 0:1]),
                     lambda e, kt=kt, tmp=tmp: e.activation(out=hT[:, kt, TX:T], in_=tmp[:, TX:T], func=AF.Identity, scale=gp[:, kt, 1:2], bias=sh[:, kt, 1:2])],
             reads=[("tmpn", kt % 2), tag + "gp", tag + "sh"], writes=[("hT", kt)])


def emit_ffn(C, wg_d, wu_d, wd_d, hT, xT, hg, tag):
    P = C.P
    NQ = 4
    FQ = FT // NQ
    wg_v = wg_d.ap().rearrange("(kt p) f -> p kt f", p=128)
    wu_v = wu_d.ap().rearrange("(kt p) f -> p kt f", p=128)
    wd_v = wd_d.ap().rearrange("(ft p) d -> p ft d", p=128)
    cnt = [0]

    def load_w(view_ap, n_in, key):
        i = cnt[0] % 2
        cnt[0] += 1
        st, wb = C.wst[i], C.wbf[i]
        P.dma("sp", st[:, 0:n_in, :], view_ap, writes=[("wst", i)])
        P.op("pool", lambda e: e.tensor_copy(out=wb[:, 0:n_in, :], in_=st[:, 0:n_in, :]), reads=[("wst", i)], writes=[("wbf", i)])
        return i, wb

    for q in range(NQ):
        for fl in range(FQ):
            ft = q * FQ + fl
            gi, wgb = load_w(wg_v[:, :, ft * 128:(ft + 1) * 128], KT, "g")
            ui, wub = load_w(wu_v[:, :, ft * 128:(ft + 1) * 128], KT, "u")
            for ci, (a, b) in enumerate(CH):
                n = b - a
                bg, pg = C.bank()
                bu, pu = C.bank()
                P.op("pe", [lambda e, kt=kt, pg=pg, wgb=wgb, a=a, b=b, n=n: e.matmul(pg[:, 0:n], lhsT=wgb[:, kt, :], rhs=hT[:, kt, a:b], start=(kt == 0), stop=(kt == KT - 1)) for kt in range(KT)],
                     reads=[("wbf", gi)] + [("hT", kt) for kt in range(KT)], writes=[("bank", bg)])
                P.op("pe", [lambda e, kt=kt, pu=pu, wub=wub, a=a, b=b, n=n: e.matmul(pu[:, 0:n], lhsT=wub[:, kt, :], rhs=hT[:, kt, a:b], start=(kt == 0), stop=(kt == KT - 1)) for kt in range(KT)],
                     reads=[("wbf", ui)] + [("hT", kt) for kt in range(KT)], writes=[("bank", bu)])
                sg = C.sg[ci % 2]
                P.op("act", lambda e, sg=sg, pg=pg, n=n: e.activation(out=sg[:, 0:n], in_=pg[:, 0:n], func=AF.Silu),
                     reads=[("bank", bg)], writes=[("sg", ci % 2)])
                P.op("dve", lambda e, sg=sg, pu=pu, n=n, fl=fl, a=a, b=b: e.tensor_tensor(out=C.actT[:, fl, a:b], in0=sg[:, 0:n], in1=pu[:, 0:n], op=ALU.mult),
                     reads=[("sg", ci % 2), ("bank", bu)], writes=[("actT", fl, ci)])
        for dt_ in range(KT):
            di, wdb = load_w(wd_v[:, q * FQ:(q + 1) * FQ, dt_ * 128:(dt_ + 1) * 128], FQ, "d")
            for ci, (a, b) in enumerate(CH):
                n = b - a
                bo, po = C.bank()
                P.op("pe", [lambda e, fl=fl, po=po, wdb=wdb, a=a, b=b, n=n: e.matmul(po[:, 0:n], lhsT=wdb[:, fl, :], rhs=C.actT[:, fl, a:b], start=(fl == 0), stop=(fl == FQ - 1)) for fl in range(FQ)],
                     reads=[("wbf", di)] + [("actT", fl, ci) for fl in range(FQ)], writes=[("bank", bo)])
                r = 0 if ci < 2 else 1
                P.op("dve", lambda e, po=po, n=n, a=a, b=b, dt_=dt_, r=r: e.scalar_tensor_tensor(out=xT[:, dt_, a:b], in0=po[:, 0:n], scalar=hg[:, dt_, r:r + 1], in1=xT[:, dt_, a:b], op0=ALU.mult, op1=ALU.add),
                     reads=[("bank", bo), tag + "hg"], writes=[("xT", dt_)])


def alloc_persist(C):
    C.onesb = C.sb("onesb", [128, 128], BF16, persist=True)
    C.epsb = C.sb("epsb", [128, 1], persist=True)
    C.ident = C.sb("ident", [128, 128], persist=True)
    C.P.op("dve", lambda e: e.memset(C.onesb[:, :], 1.0), writes=["ones"])
    C.P.op("dve", lambda e: e.memset(C.epsb[:, :], EPS), writes=["epsb"])


def alloc_common(C):
    C.xT = C.sb("xT", [128, KT, T])
    C.hT = C.sb("hT", [128, KT, T], BF16)
    C.actT = C.sb("actT", [128, FT // 4, T], BF16)
    C.wst = [C.sb("wst%d" % i, [128, KT, 128]) for i in range(2)]
    C.wbf = [C.sb("wbf%d" % i, [128, KT, 128], BF16) for i in range(2)]
    C.sg = [C.sb("sg%d" % i, [128, 512]) for i in range(2)]
    C.rstd = C.sb("rstd", [128, T])
    C.tmpn = [C.sb("tmpn%d" % i, [128, T]) for i in range(2)]


def emit_mods(C, modT, gT, sub, tag):
    P = C.P
    gp = C.sb(tag + "gp", [128, KT, 2])
    hg = C.sb(tag + "hg", [128, KT, 2])
    for r in range(2):
        P.op("dve", lambda e, r=r: e.scalar_tensor_tensor(out=gp[:, :, r], in0=modT[:, sub * 3 + 1, :, r], scalar=1.0, in1=gT[:, sub, :], op0=ALU.add, op1=ALU.mult),
             reads=["modT", "gT"], writes=[tag + "gp"])
    P.op("dve", lambda e: e.tensor_scalar(out=hg[:, :, :], in0=modT[:, sub * 3 + 2, :, :], scalar1=0.5, scalar2=None, op0=ALU.mult),
         reads=["modT"], writes=[tag + "hg"])
    sh = C.sb(tag + "sh", [128, KT, 2])
    P.op("dve", lambda e: e.tensor_copy(out=sh[:, :, :], in_=modT[:, sub * 3 + 0, :, :]), reads=["modT"], writes=[tag + "sh"])
    return gp, sh, hg


def build_L1():
    nc = bass.Bass("TRN2", target_bir_lowering=False)
    x_tok = nc.dram_tensor("x_tok", [T, D], F32, kind="ExternalInput")
    scin = nc.dram_tensor("scin", [128, KT, 2], F32, kind="ExternalInput")
    w_ada = nc.dram_tensor("w_ada", [D, 9 * D], F32, kind="ExternalInput")
    bT_d = nc.dram_tensor("bT", [128, 9, KT], F32, kind="ExternalInput")
    gT_d = nc.dram_tensor("gT", [128, 3, KT], F32, kind="ExternalInput")
    wg = nc.dram_tensor("wg", [D, DFF], F32, kind="ExternalInput")
    wu = nc.dram_tensor("wu", [D, DFF], F32, kind="ExternalInput")
    wd = nc.dram_tensor("wd", [DFF, D], F32, kind="ExternalInput")
    ident_d = nc.dram_tensor("ident", [128, 128], F32, kind="ExternalInput")
    modT_o = nc.dram_tensor("modT_o", [128, 9, KT, 2], F32, kind="ExternalOutput")
    xT_o = nc.dram_tensor("xT_o", [128, KT, T], F32, kind="ExternalOutput")
    hT_o = nc.dram_tensor("hT_o", [128, KT, T], BF16, kind="ExternalOutput")
    with ExitStack() as ctx:
        C = Ctx(nc, ctx)
        P = C.P
        alloc_persist(C)
        C.phase()
        alloc_common(C)
        P.dma("sp", C.ident[:, :], ident_d[:, :], writes=["ident"])
        sc = C.sb("sc", [128, KT, 2])
        bT = C.sb("bTs", [128, 9, KT])
        gT = C.sb("gTs", [128, 3, KT])
        modT = C.sb("modT", [128, 9, KT, 2])
        P.dma("sp", sc[:, :, :], scin[:, :, :], writes=["sc"])
        P.dma("sp", bT[:, :, :], bT_d[:, :, :], writes=["bT"])
        P.dma("sp", gT[:, :, :], gT_d[:, :, :], writes=["gT"])
        P.op("act", lambda e: e.activation(out=sc[:, :, :], in_=sc[:, :, :], func=AF.Silu), reads=["sc"], writes=["sc"])
        wa_v = w_ada.ap().rearrange("(kt p) c -> p kt c", p=128)
        for cb in range(9 * D // 128):
            i = cb % 2
            st = C.wst[i]
            P.dma("sp", st[:, :, :], wa_v[:, :, cb * 128:(cb + 1) * 128], writes=[("wst", i)])
            bi, bk = C.bank()
            P.op("pe", [lambda e, kt=kt, st=st, bk=bk: e.matmul(bk[:, 0:2], lhsT=st[:, kt, :], rhs=sc[:, kt, :], start=(kt == 0), stop=(kt == KT - 1)) for kt in range(KT)],
                 reads=[("wst", i), "sc"], writes=[("bank", bi)])
            blk, k2 = cb // KT, cb % KT
            P.op("dve", lambda e, bk=bk, blk=blk, k2=k2: e.tensor_scalar(out=modT[:, blk, k2, :], in0=bk[:, 0:2], scalar1=bT[:, blk, k2:k2 + 1], scalar2=None, op0=ALU.add),
                 reads=[("bank", bi), "bT"], writes=["modT"])
        P.dma("sp", modT_o[:, :, :, :], modT[:, :, :, :], reads=["modT"], writes=["modT_o"])
        xs = [C.sb("xs%d" % i, [128, D]) for i in range(2)]
        for tt in range(9):
            nt = 128 if tt < 8 else 64
            i = tt % 2
            P.dma("sp", xs[i][0:nt, :], x_tok[tt * 128:tt * 128 + nt, :], writes=[("xs", i)])
            for k4 in range(4):
                bi, bk = C.bank()
                P.op("pe", [lambda e, j=j, k4=k4, bk=bk, i=i, nt=nt: e.transpose(out=bk[:, j * 128:j * 128 + nt], in_=xs[i][0:nt, (k4 * 4 + j) * 128:(k4 * 4 + j + 1) * 128], identity=C.ident[0:nt, 0:nt]) for j in range(4)],
                     reads=[("xs", i), "ident"], writes=[("bank", bi)])
                for j in range(4):
                    kt = k4 * 4 + j
                    P.op("act", lambda e, j=j, kt=kt, bk=bk, nt=nt, tt=tt: e.copy(out=C.xT[:, kt, tt * 128:tt * 128 + nt], in_=bk[:, j * 128:j * 128 + nt]),
                         reads=[("bank", bi)], writes=[("xT", kt)])
        gp, sh, hg = emit_mods(C, modT, gT, 0, "s0")
        emit_norm_mod(C, C.xT, C.hT, None, C.rstd, C.onesb, gp, sh, "s0")
        emit_ffn(C, wg, wu, wd, C.hT, C.xT, hg, "s0")
        gp1, sh1, _ = emit_mods(C, modT, gT, 1, "s1")
        emit_norm_mod(C, C.xT, C.hT, None, C.rstd, C.onesb, gp1, sh1, "s1")
        P.dma("sp", xT_o[:, :, :], C.xT[:, :, :], reads=[("xT", kt) for kt in range(KT)], writes=["xT_o"])
        P.dma("sp", hT_o[:, :, :], C.hT[:, :, :], reads=[("hT", kt) for kt in range(KT)], writes=["hT_o"])
        P.wait_all("sp", ["modT_o", "xT_o", "hT_o"])
        P.replay()
    return nc


LC = 512
CTXL = 256
SEQ = 4096
TWO_PI = 6.283185307179586


def mkap(t, offset, dims):
    dims = [list(d) for d in dims]
    if isinstance(t, bass.AP):
        return bass.AP(t.tensor, t.offset + offset, dims)
    return bass.AP(t, offset, dims)


def emit_sincos(C, ang, n, sin_out, cos_out, tag):
    P = C.P
    ki = C.sc_ki[:, 0:n]
    kf = C.sc_kf[:, 0:n]
    a2 = C.sc_a2[:, 0:n]
    for which, out in (("s", sin_out), ("c", cos_out)):
        src = ang
        if which == "c":
            P.op("dve", lambda e: e.tensor_scalar(out=a2, in0=ang, scalar1=float(np.pi / 2), scalar2=None, op0=ALU.add),
                 reads=[tag + "ang"], writes=["sc_a2"])
            src = a2
        P.op("dve", lambda e, src=src: e.tensor_scalar(out=ki, in0=src, scalar1=float(1.0 / TWO_PI), scalar2=None, op0=ALU.mult),
             reads=[tag + "ang", "sc_a2"], writes=["sc_ki"])
        P.op("dve", lambda e: e.tensor_copy(out=kf, in_=ki), reads=["sc_ki"], writes=["sc_kf"])
        P.op("dve", lambda e, src=src: e.scalar_tensor_tensor(out=kf, in0=kf, scalar=float(-TWO_PI), in1=src, op0=ALU.mult, op1=ALU.add),
             reads=["sc_kf", tag + "ang", "sc_a2"], writes=["sc_kf"])
        P.op("dve", lambda e: e.tensor_scalar(out=kf, in0=kf, scalar1=-3.1415925, scalar2=3.1415925, op0=ALU.max, op1=ALU.min),
             reads=["sc_kf"], writes=["sc_kf"])
        P.op("act", lambda e, out=out: e.activation(out=out, in_=kf, func=AF.Sin), reads=["sc_kf"], writes=[tag + which])


def s5_setup(C, pp_d, row_d, braw_d, craw_d, dsk_d, tau_d):
    P = C.P
    S = {}
    BbT = C.sb("s5BbT", [128, 8, 2, 128], BF16)
    CT = C.sb("s5CT", [128, 8, 2, 128], BF16)
    sinT = C.sb("s5sinT", [128, 8, LC])
    cosT = C.sb("s5cosT", [128, 8, LC])
    dsk = C.sb("s5dsk", [128, 1])
    rp = C.sb("s5rp", [128, 8])
    s5mark = C.ptr
    C.sc_ki = C.sb("sc_ki", [128, 1024], I32)
    C.sc_kf = C.sb("sc_kf", [128, 1024])
    C.sc_a2 = C.sb("sc_a2", [128, 1024])
    pp = C.sb("s5pp", [128, 3, 8])
    P.dma("sp", pp[:, :, :], pp_d.ap().rearrange("p w d k -> p w (d k)"), writes=["s5pp"])
    row = C.sb("s5row", [128, 3, 1024])
    P.dma("sp", row[:, :, :], row_d.ap().rearrange("p w d k c -> p w (d k c)"), writes=["s5row"])
    braw = C.sb("s5braw", [128, 8, 2, 128])
    P.dma("sp", braw[:, :, :, :], braw_d.ap().rearrange("p d k r c -> p (d k) r c"), writes=["s5braw"])
    craw = C.sb("s5craw", [128, 8, 2, 128])
    P.dma("sp", craw[:, :, :, :], craw_d.ap().rearrange("p d k r c -> p (d k) r c"), writes=["s5craw"])
    P.dma("sp", dsk[:, :], dsk_d[:, :], writes=["s5dsk"])
    tau = C.sb("s5tau", [128, LC])
    P.dma("sp", tau[:, :], tau_d[:, :], writes=["s5tau"])
    dtp = C.sb("s5dtp", [128, 8])
    thp = C.sb("s5thp", [128, 8])
    P.op("act", lambda e: e.activation(out=dtp[:, :], in_=pp[:, 2, :], func=AF.Exp), reads=["s5pp"], writes=["s5dtp"])
    P.op("dve", lambda e: e.tensor_scalar(out=rp[:, :], in0=pp[:, 0, :], scalar1=-1e-4, scalar2=None, op0=ALU.min), reads=["s5pp"], writes=["s5rp"])
    P.op("dve", lambda e: e.tensor_tensor(out=rp[:, :], in0=rp[:, :], in1=dtp[:, :], op=ALU.mult), reads=["s5rp", "s5dtp"], writes=["s5rp"])
    P.op("act", lambda e: e.activation(out=rp[:, :], in_=rp[:, :], func=AF.Exp), reads=["s5rp"], writes=["s5rp"])
    P.op("dve", lambda e: e.tensor_tensor(out=thp[:, :], in0=pp[:, 1, :], in1=dtp[:, :], op=ALU.mult), reads=["s5pp", "s5dtp"], writes=["s5thp"])
    W = [C.sb("s5w%d" % i, [128, 1024]) for i in range(6)]
    dtr, lr, th, mag, sn, cs = W
    li = row[:, 1, :]
    P.op("act", lambda e: e.activation(out=dtr[:, :], in_=row[:, 2, :], func=AF.Exp), reads=["s5row"], writes=["w0"])
    P.op("dve", lambda e: e.tensor_scalar(out=lr[:, :], in0=row[:, 0, :], scalar1=-1e-4, scalar2=None, op0=ALU.min), reads=["s5row"], writes=["w1"])
    P.op("dve", lambda e: e.tensor_tensor(out=th[:, :], in0=li, in1=dtr[:, :], op=ALU.mult), reads=["s5row", "w0"], writes=["rowang"])
    P.op("dve", lambda e: e.tensor_tensor(out=mag[:, :], in0=lr[:, :], in1=dtr[:, :], op=ALU.mult), reads=["w1", "w0"], writes=["w3"])
    P.op("act", lambda e: e.activation(out=mag[:, :], in_=mag[:, :], func=AF.Exp), reads=["w3"], writes=["w3"])
    emit_sincos(C, th[:, :], 1024, sn[:, :], cs[:, :], "row")
    P.op("dve", lambda e: e.tensor_tensor(out=cs[:, :], in0=cs[:, :], in1=mag[:, :], op=ALU.mult), reads=["rowc", "w3"], writes=["rowc"])
    P.op("dve", lambda e: e.tensor_scalar(out=cs[:, :], in0=cs[:, :], scalar1=-1.0, scalar2=None, op0=ALU.add), reads=["rowc"], writes=["rowc"])
    P.op("dve", lambda e: e.tensor_tensor(out=sn[:, :], in0=sn[:, :], in1=mag[:, :], op=ALU.mult), reads=["rows", "w3"], writes=["rows"])
    P.op("dve", lambda e: e.tensor_tensor(out=mag[:, :], in0=lr[:, :], in1=lr[:, :], op=ALU.mult), reads=["w1", "w3"], writes=["w3"])
    P.op("dve", lambda e: e.tensor_tensor(out=dtr[:, :], in0=li, in1=li, op=ALU.mult), reads=["s5row", "w0", "rowang"], writes=["w0"])
    P.op("dve", lambda e: e.tensor_tensor(out=mag[:, :], in0=mag[:, :], in1=dtr[:, :], op=ALU.add), reads=["w0", "w3"], writes=["w3"])
    P.op("dve", lambda e: e.reciprocal(out=mag[:, :], in_=mag[:, :]), reads=["w3"], writes=["w3"])
    P.op("dve", lambda e: e.tensor_tensor(out=dtr[:, :], in0=cs[:, :], in1=lr[:, :], op=ALU.mult), reads=["rowc", "w1", "w0"], writes=["w0"])
    P.op("dve", lambda e: e.tensor_tensor(out=th[:, :], in0=sn[:, :], in1=li, op=ALU.mult), reads=["rows", "s5row", "rowang", "sc_a2", "sc_kf"], writes=["rowang"])
    P.op("dve", lambda e: e.tensor_tensor(out=dtr[:, :], in0=dtr[:, :], in1=th[:, :], op=ALU.add), reads=["w0", "rowang"], writes=["w0"])
    P.op("dve", lambda e: e.tensor_tensor(out=dtr[:, :], in0=dtr[:, :], in1=mag[:, :], op=ALU.mult), reads=["w0", "w3"], writes=["w0"])
    P.op("dve", lambda e: e.tensor_tensor(out=th[:, :], in0=sn[:, :], in1=lr[:, :], op=ALU.mult), reads=["rows", "w1", "rowang"], writes=["rowang"])
    P.op("dve", lambda e: e.tensor_tensor(out=sn[:, :], in0=cs[:, :], in1=li, op=ALU.mult), reads=["rowc", "s5row", "rows", "rowang"], writes=["rows"])
    P.op("dve", lambda e: e.tensor_tensor(out=th[:, :], in0=th[:, :], in1=sn[:, :], op=ALU.subtract), reads=["rows", "rowang"], writes=["rowang"])
    P.op("dve", lambda e: e.tensor_tensor(out=th[:, :], in0=th[:, :], in1=mag[:, :], op=ALU.mult), reads=["rowang", "w3"], writes=["rowang"])
    fre = dtr[:, :].rearrange("p (a c) -> p a c", c=128)
    fim = th[:, :].rearrange("p (a c) -> p a c", c=128)
    t1 = cs[:, :].rearrange("p (a c) -> p a c", c=128)
    t2 = sn[:, :].rearrange("p (a c) -> p a c", c=128)
    rk = ["w0", "rowang", "s5braw"]
    P.op("dve", lambda e: e.tensor_tensor(out=t1, in0=fre, in1=braw[:, :, 0, :], op=ALU.mult), reads=rk + ["rowc"], writes=["rowc"])
    P.op("dve", lambda e: e.tensor_tensor(out=t2, in0=fim, in1=braw[:, :, 1, :], op=ALU.mult), reads=rk + ["rows"], writes=["rows"])
    P.op("dve", lambda e: e.tensor_tensor(out=BbT[:, :, 0, :], in0=t1, in1=t2, op=ALU.subtract), reads=["rowc", "rows"], writes=["s5BbT"])
    P.op("dve", lambda e: e.tensor_tensor(out=t1, in0=fre, in1=braw[:, :, 1, :], op=ALU.mult), reads=rk + ["rowc"], writes=["rowc"])
    P.op("dve", lambda e: e.tensor_tensor(out=t2, in0=fim, in1=braw[:, :, 0, :], op=ALU.mult), reads=rk + ["rows"], writes=["rows"])
    P.op("dve", lambda e: e.tensor_tensor(out=BbT[:, :, 1, :], in0=t1, in1=t2, op=ALU.add), reads=["rowc", "rows", "s5BbT"], writes=["s5BbT"])
    P.op("dve", lambda e: e.tensor_copy(out=CT[:, :, 0, :], in_=craw[:, :, 0, :]), reads=["s5craw"], writes=["s5CT"])
    P.op("dve", lambda e: e.tensor_scalar(out=CT[:, :, 1, :], in0=craw[:, :, 1, :], scalar1=-1.0, scalar2=None, op0=ALU.mult), reads=["s5craw", "s5CT"], writes=["s5CT"])
    for dk in range(8):
        ang = W[1][:, 0:LC]
        P.op("dve", lambda e, dk=dk, ang=ang: e.tensor_scalar(out=ang, in0=tau[:, :], scalar1=thp[:, dk:dk + 1], scalar2=None, op0=ALU.mult),
             reads=["s5tau", "s5thp", "w1", "sc_a2", "sc_kf"], writes=["tabang"])
        emit_sincos(C, ang, LC, sinT[:, dk, :], cosT[:, dk, :], "tab")
        P.op("dve", lambda e: e.engine_nop(), reads=["tabs", "tabc"], writes=["s5tabs"])
    S.update(rp=rp, BbT=BbT, CT=CT, sinT=sinT, cosT=cosT, dsk=dsk)
    P.barrier()
    C.ptr = s5mark
    return S


def emit_s5(C, S, uT, ys5T):
    P = C.P
    rp, BbT, CT, sinT, cosT, dsk = S["rp"], S["BbT"], S["CT"], S["sinT"], S["cosT"], S["dsk"]
    NW = 2
    wk = [[C.sb("s5t%d_%d" % (s, i), [128, LC]) for i in range(8)] for s in range(NW)]
    sre = [C.sb("s5sre%d" % s, [128, 4, LC], BF16) for s in range(2)]
    sim = [C.sb("s5sim%d" % s, [128, 4, LC], BF16) for s in range(2)]
    car = C.sb("s5car", [128, 2, 4, 2])
    ybw = C.sb("s5ybw", [128, SEQ])
    ytmp = C.sb("s5ytmp", [128, LC])
    unit = [0]
    chunkc = [0]
    for b in range(2):
        for d in (1, 0):
            P.op("dve", lambda e: e.memset(car[:, 0, :, :], 0.0), reads=[], writes=[("car", k) for k in range(4)])
            if d == 0:
                chunks = [(0, CTXL, False)] + [(CTXL + c * LC, LC, True) for c in range(SEQ // LC)]
            else:
                chunks = [(0, CTXL, False)] + [(CTXL + c * LC, LC, True) for c in reversed(range(SEQ // LC))]
            for (t0, L, isl) in chunks:
                cs_ = chunkc[0] % 2
                chunkc[0] += 1
                for k in range(4):
                    dk = d * 4 + k
                    s = unit[0] % NW
                    unit[0] += 1
                    m1, m2, m3, m4, bre, bim, qre, qim = [w[:, 0:L] for w in wk[s]]
                    wkeys = [("wk", s, i) for i in range(8)]
                    bi_r, bk_r = C.bank()
                    bi_i, bk_i = C.bank()
                    P.op("pe", lambda e, bk_r=bk_r, dk=dk, b=b, t0=t0, L=L: e.matmul(bk_r[:, 0:L], lhsT=BbT[:, dk, 0, :], rhs=uT[:, b, t0:t0 + L], start=True, stop=True),
                         reads=["s5BbT", ("uT", b)], writes=[("bank", bi_r)])
                    P.op("pe", lambda e, bk_i=bk_i, dk=dk, b=b, t0=t0, L=L: e.matmul(bk_i[:, 0:L], lhsT=BbT[:, dk, 1, :], rhs=uT[:, b, t0:t0 + L], start=True, stop=True),
                         reads=["s5BbT", ("uT", b)], writes=[("bank", bi_i)])
                    pstep = list(bk_r[:, :].ap[0])
                    if d == 0:
                        bur, bui = bk_r[:, 0:L], bk_i[:, 0:L]
                    else:
                        bur = mkap(bk_r, L - 1, [pstep, [-1, L]])
                        bui = mkap(bk_i, L - 1, [pstep, [-1, L]])
                    cT_, sT_ = cosT[:, dk, 0:L], sinT[:, dk, 0:L]
                    P.op("dve", lambda e, m1=m1, bur=bur, cT_=cT_: e.tensor_tensor(out=m1, in0=bur, in1=cT_, op=ALU.mult), reads=[("bank", bi_r), "s5tabs"], writes=[wkeys[0]])
                    P.op("dve", lambda e, m2=m2, bui=bui, sT_=sT_: e.tensor_tensor(out=m2, in0=bui, in1=sT_, op=ALU.mult), reads=[("bank", bi_i), "s5tabs"], writes=[wkeys[1]])
                    P.op("dve", lambda e, m3=m3, bui=bui, cT_=cT_: e.tensor_tensor(out=m3, in0=bui, in1=cT_, op=ALU.mult), reads=[("bank", bi_i), "s5tabs"], writes=[wkeys[2]])
                    P.op("dve", lambda e, m4=m4, bur=bur, sT_=sT_: e.tensor_tensor(out=m4, in0=bur, in1=sT_, op=ALU.mult), reads=[("bank", bi_r), "s5tabs"], writes=[wkeys[3]])
                    P.op("pool", lambda e, bre=bre, m1=m1, m2=m2: e.tensor_tensor(out=bre, in0=m1, in1=m2, op=ALU.add), reads=[wkeys[0], wkeys[1]], writes=[wkeys[4]])
                    P.op("pool", lambda e, bim=bim, m3=m3, m4=m4: e.tensor_tensor(out=bim, in0=m3, in1=m4, op=ALU.subtract), reads=[wkeys[2], wkeys[3]], writes=[wkeys[5]])
                    rb = mkap(rp, dk, [list(rp[:, :].ap[0]), [0, L]])
                    P.op("dve", lambda e, qre=qre, bre=bre, rb=rb, k=k: e.tensor_tensor_scan(out=qre, data0=rb, data1=bre, initial=car[:, 0, k, 0:1], op0=ALU.mult, op1=ALU.add),
                         reads=[wkeys[4], "s5rp", ("car", k)], writes=[wkeys[6]])
                    P.op("dve", lambda e, qim=qim, bim=bim, rb=rb, k=k: e.tensor_tensor_scan(out=qim, data0=rb, data1=bim, initial=car[:, 0, k, 1:2], op0=ALU.mult, op1=ALU.add),
                         reads=[wkeys[5], "s5rp", ("car", k)], writes=[wkeys[7]])
                    cl, sl = cosT[:, dk, L - 1:L], sinT[:, dk, L - 1:L]
                    qrl, qil = wk[s][6][:, L - 1:L], wk[s][7][:, L - 1:L]
                    P.op("dve", lambda e, k=k, qil=qil, sl=sl: e.tensor_tensor(out=car[:, 1, k, 0:1], in0=qil, in1=sl, op=ALU.mult), reads=[wkeys[7], "s5tabs"], writes=[("car1", k)])
                    P.op("dve", lambda e, k=k, qil=qil, cl=cl: e.tensor_tensor(out=car[:, 1, k, 1:2], in0=qil, in1=cl, op=ALU.mult), reads=[wkeys[7], "s5tabs", ("car1", k)], writes=[("car1", k)])
                    P.op("dve", lambda e, k=k, qrl=qrl, cl=cl: e.scalar_tensor_tensor(out=car[:, 0, k, 0:1], in0=qrl, scalar=cl, in1=car[:, 1, k, 0:1], op0=ALU.mult, op1=ALU.subtract),
                         reads=[wkeys[6], ("car1", k), "s5tabs"], writes=[("car", k)])
                    P.op("dve", lambda e, k=k, qrl=qrl, sl=sl: e.scalar_tensor_tensor(out=car[:, 0, k, 1:2], in0=qrl, scalar=sl, in1=car[:, 1, k, 1:2], op0=ALU.mult, op1=ALU.add),
                         reads=[wkeys[6], ("car1", k), "s5tabs", ("car", k)], writes=[("car", k)])
                    if not isl:
                        continue
                    if d == 0:
                        so_r, so_i = sre[cs_][:, k, 0:L], sim[cs_][:, k, 0:L]
                    else:
                        ps_ = list(sre[cs_][:, :, :].ap[0])
                        so_r = mkap(sre[cs_], k * LC + L - 1, [ps_, [-1, L]])
                        so_i = mkap(sim[cs_], k * LC + L - 1, [ps_, [-1, L]])
                    P.op("pool", lambda e, m1=m1, qre=qre, cT_=cT_: e.tensor_tensor(out=m1, in0=qre, in1=cT_, op=ALU.mult), reads=[wkeys[6], "s5tabs"], writes=[wkeys[0]])
                    P.op("pool", lambda e, m2=m2, qim=qim, sT_=sT_: e.tensor_tensor(out=m2, in0=qim, in1=sT_, op=ALU.mult), reads=[wkeys[7], "s5tabs"], writes=[wkeys[1]])
                    P.op("pool", lambda e, so_r=so_r, m1=m1, m2=m2: e.tensor_tensor(out=so_r, in0=m1, in1=m2, op=ALU.subtract), reads=[wkeys[0], wkeys[1]], writes=[("sre", cs_, k)])
                    P.op("dve", lambda e, m3=m3, qre=qre, sT_=sT_: e.tensor_tensor(out=m3, in0=qre, in1=sT_, op=ALU.mult), reads=[wkeys[6], "s5tabs"], writes=[wkeys[2]])
                    P.op("dve", lambda e, m4=m4, qim=qim, cT_=cT_: e.tensor_tensor(out=m4, in0=qim, in1=cT_, op=ALU.mult), reads=[wkeys[7], "s5tabs"], writes=[wkeys[3]])
                    P.op("dve", lambda e, so_i=so_i, m3=m3, m4=m4: e.tensor_tensor(out=so_i, in0=m3, in1=m4, op=ALU.add), reads=[wkeys[2], wkeys[3]], writes=[("sim", cs_, k)])
                if not isl:
                    continue
                bi_y, bk_y = C.bank()
                fns = []
                for k in range(4):
                    dk = d * 4 + k
                    fns.append(lambda e, k=k, dk=dk, bk_y=bk_y, cs_=cs_: e.matmul(bk_y[:, 0:LC], lhsT=CT[:, dk, 0, :], rhs=sre[cs_][:, k, :], start=(k == 0), stop=False))
                    fns.append(lambda e, k=k, dk=dk, bk_y=bk_y, cs_=cs_: e.matmul(bk_y[:, 0:LC], lhsT=CT[:, dk, 1, :], rhs=sim[cs_][:, k, :], start=False, stop=(k == 3)))
                P.op("pe", fns, reads=["s5CT"] + [("sre", cs_, k) for k in range(4)] + [("sim", cs_, k) for k in range(4)], writes=[("bank", bi_y)])
                tl = t0 - CTXL
                if d == 1:
                    P.op("act", lambda e, bk_y=bk_y, tl=tl: e.copy(out=ybw[:, tl:tl + LC], in_=bk_y[:, 0:LC]), reads=[("bank", bi_y)], writes=[("ybw", tl)])
                else:
                    P.op("dve", lambda e, bk_y=bk_y, tl=tl: e.tensor_tensor(out=ytmp[:, :], in0=bk_y[:, 0:LC], in1=ybw[:, tl:tl + LC], op=ALU.add),
                         reads=[("bank", bi_y), ("ybw", tl)], writes=["ytmp"])
                    P.op("dve", lambda e, tl=tl, b=b, t0=t0: e.scalar_tensor_tensor(out=ys5T[:, b, tl:tl + LC], in0=uT[:, b, t0:t0 + LC], scalar=dsk[:, 0:1], in1=ytmp[:, :], op0=ALU.mult, op1=ALU.add),
                         reads=["ytmp", ("uT", b), "s5dsk"], writes=[("ys5T", b)])


def s5_host_inputs(inp, j):
    G0 = 8 * j
    lam_re, lam_im, log_dt = inp["s5_lam_re"][0], inp["s5_lam_im"][0], inp["s5_log_dt"][0]
    b_re, b_im, c_re, c_im = inp["s5_b_re"][0], inp["s5_b_im"][0], inp["s5_c_re"][0], inp["s5_c_im"][0]
    pp = np.zeros((128, 3, 2, 4), np.float32)
    braw = np.zeros((128, 2, 4, 2, 128), np.float32)
    craw = np.zeros((128, 2, 4, 2, 128), np.float32)
    for d in range(2):
        for k in range(4):
            for g2 in range(2):
                g = G0 + 2 * k + g2
                gl = 2 * k + g2
                sl = slice(64 * g2, 64 * g2 + 64)
                pp[sl, 0, d, k] = lam_re[d, g]
                pp[sl, 1, d, k] = lam_im[d, g]
                pp[sl, 2, d, k] = log_dt[d, g]
                braw[16 * gl:16 * gl + 16, d, k, 0, sl] = b_re[d, g].T
                braw[16 * gl:16 * gl + 16, d, k, 1, sl] = b_im[d, g].T
                craw[sl, d, k, 0, 16 * gl:16 * gl + 16] = c_re[d, g].T
                craw[sl, d, k, 1, 16 * gl:16 * gl + 16] = c_im[d, g].T
    row = np.broadcast_to(pp.transpose(1, 2, 3, 0)[None], (128, 3, 2, 4, 128)).copy()
    dsk = inp["s5_d"][0, 128 * j:128 * j + 128].reshape(128, 1).astype(np.float32)
    tau = np.broadcast_to(np.arange(1, LC + 1, dtype=np.float32)[None], (128, LC)).copy()
    return {"s5pp": pp, "s5row": row, "s5braw": braw, "s5craw": craw, "s5dsk": dsk, "s5tau": tau}


HL = 4096
NCHK = HL // 512


def emit_sin_rr(C, arg, ki, kf, out, rkeys, wkey):
    P = C.P
    P.op("dve", lambda e: e.tensor_scalar(out=ki, in0=arg, scalar1=float(1.0 / TWO_PI), scalar2=None, op0=ALU.mult), reads=rkeys, writes=["rr_ki"])
    P.op("dve", lambda e: e.tensor_copy(out=kf, in_=ki), reads=["rr_ki"], writes=["rr_kf"])
    P.op("dve", lambda e: e.scalar_tensor_tensor(out=kf, in0=kf, scalar=float(-TWO_PI), in1=arg, op0=ALU.mult, op1=ALU.add), reads=["rr_kf"] + rkeys, writes=["rr_kf"])
    P.op("dve", lambda e: e.tensor_scalar(out=kf, in0=kf, scalar1=-3.1415925, scalar2=3.1415925, op0=ALU.max, op1=ALU.min), reads=["rr_kf"], writes=["rr_kf"])
    P.op("act", lambda e: e.activation(out=out, in_=kf, func=AF.Sin), reads=["rr_kf"], writes=[wkey])


def hy_filters(C, Dm, Kscr):
    P = C.P
    C.phase()
    feats = C.sb("hy_feats", [33, HL])
    w1 = C.sb("hy_w1", [33, 64])
    w2 = C.sb("hy_w2", [64, 64])
    w3 = C.sb("hy_w3", [64, 64])
    hcol = C.sb("hy_col", [64, 4])
    fb = C.sb("hy_fb", [64, 3])
    wout = C.sb("hy_wout", [64, 4, 128])
    nd = C.sb("hy_nd", [128, 4])
    nb = C.sb("hy_nb", [128, 4, NCHK])
    tau = C.sb("hy_tau", [128, 512])
    Rm = C.sb("hy_Rm", [128, 256])
    hA = C.sb("hy_hA", [64, HL])
    P.dma("sp", feats[:, :], Dm["hy_feats"][:, :], writes=["hy_feats"])
    P.dma("sp", w1[:, :], Dm["hy_w1"][:, :], writes=["hy_w"])
    P.dma("sp", w2[:, :], Dm["hy_w2"][:, :], writes=["hy_w"])
    P.dma("sp", w3[:, :], Dm["hy_w3"][:, :], writes=["hy_w"])
    P.dma("sp", hcol[:, :], Dm["hy_col"][:, :], writes=["hy_w"])
    P.dma("sp", wout[:, :, :], Dm["hy_wout"][:, :, :], writes=["hy_w"])
    P.dma("sp", nd[:, :], Dm["hy_nd"][:, :], writes=["hy_w"])
    P.dma("sp", nb[:, :, :], Dm["hy_nb"][:, :, :], writes=["hy_w"])
    P.dma("sp", tau[:, :], Dm["tau"][:, :], writes=["hy_w"])
    P.dma("sp", Rm[:, :], Dm["hy_Rm"][:, :], writes=["hy_w"])
    P.dma("sp", C.ident[:, :], Dm["ident"][:, :], writes=["ident"])
    for i in range(3):
        P.op("dve", lambda e, i=i: e.tensor_tensor(out=fb[:, i:i + 1], in0=hcol[:, 0:1], in1=hcol[:, i + 1:i + 2], op=ALU.mult), reads=["hy_w"], writes=["hy_fb"])
    mark = C.ptr
    hB = C.sb("hy_hB", [64, HL])
    arg = C.sb("hy_arg", [64, 512])
    ki = C.sb("hy_ki", [64, 512], I32)
    kf = C.sb("hy_kf", [64, 512])
    layers = [(w1, 33, feats, hA, "hA"), (w2, 64, hA, hB, "hB"), (w3, 64, hB, hA, "hA")]
    for li, (w, K, src, dst, dk) in enumerate(layers):
        for c in range(NCHK):
            bi, bk = C.bank()
            P.op("pe", lambda e, w=w, K=K, src=src, c=c, bk=bk: e.matmul(bk[0:64, 0:512], lhsT=w[0:K, :], rhs=src[0:K, c * 512:(c + 1) * 512], start=True, stop=True),
                 reads=["hy_w", "hy_feats", ("hA", c), ("hB", c)], writes=[("bank", bi)])
            P.op("dve", lambda e, bk=bk, li=li: e.tensor_scalar(out=arg[:, :], in0=bk[0:64, 0:512], scalar1=hcol[:, 0:1], scalar2=fb[:, li:li + 1], op0=ALU.mult, op1=ALU.add),
                 reads=[("bank", bi), "hy_w", "hy_fb"], writes=["hy_arg"])
            emit_sin_rr(C, arg[:, :], ki[:, :], kf[:, :], dst[:, c * 512:(c + 1) * 512], ["hy_arg"], (dk, c))
    C.P.barrier()
    C.ptr = mark
    kT = C.sb("hy_kT", [128, 2, HL])
    Df = C.sb("hy_Df", [64, 64, 64])
    Y = [C.sb("hy_Y%d" % i, [64, 2, 128, 64], BF16) for i in range(2)]
    kst = [C.sb("hy_kst%d" % i, [128, 2, 8, 64]) for i in range(2)]
    Vt = [C.sb("hy_Vt%d" % i, [64, 8, 6, 128], BF16) for i in range(2)]
    dec = C.sb("hy_dec", [128, 512])
    nrm = C.sb("hy_nrm", [128, 2])
    ev = [0]
    for o in range(2):
        for dr in range(2):
            q = o * 2 + dr
            for c in range(NCHK):
                bi, bk = C.bank()
                P.op("pe", lambda e, q=q, c=c, bk=bk: e.matmul(bk[:, 0:512], lhsT=wout[:, q, :], rhs=hA[:, c * 512:(c + 1) * 512], start=True, stop=True),
                     reads=["hy_w", ("hA", c)], writes=[("bank", bi)])
                P.op("act", lambda e, q=q, c=c: e.activation(out=dec[:, :], in_=tau[:, :], func=AF.Exp, scale=nd[:, q:q + 1], bias=nb[:, q, c:c + 1]),
                     reads=["hy_w"], writes=["hy_dec"])
                P.op("dve", lambda e, dr=dr, c=c, bk=bk: e.tensor_tensor(out=kT[:, dr, c * 512:(c + 1) * 512], in0=bk[:, 0:512], in1=dec[:, :], op=ALU.mult),
                     reads=[("bank", bi), "hy_dec"], writes=["hy_kT"])
        P.op("dve", lambda e: e.memset(kT[:, 1, 0:1], 0.0), reads=[], writes=["hy_kT"])
        kflat = kT[:, :, :].rearrange("p a b -> p (a b)")
        P.op("dve", lambda e: e.tensor_reduce(out=nrm[:, 0:1], in_=kflat, axis=AX.X, op=ALU.add, apply_absolute_value=True), reads=["hy_kT"], writes=["hy_nrm"])
        P.op("dve", lambda e: e.reciprocal(out=nrm[:, 1:2], in_=nrm[:, 0:1]), reads=["hy_nrm"], writes=["hy_nrm"])
        P.op("dve", lambda e: e.tensor_scalar(out=kflat, in0=kflat, scalar1=nrm[:, 1:2], scalar2=None, op0=ALU.mult), reads=["hy_nrm", "hy_kT"], writes=["hy_kT"])
        for hc in range(2):
            for dr in range(2):
                base = kT[64 * hc:64 * hc + 64, dr, :]
                pst = list(base.ap[0])
                for nb8 in range(8):
                    bi, bk = C.bank()
                    fns = []
                    for j in range(8):
                        n1 = nb8 * 8 + j
                        in_ = mkap(base, n1, [pst, [64, 64]])
                        fns.append(lambda e, j=j, in_=in_, bk=bk, hc=hc: e.transpose(out=bk[0:64, j * 64:(j + 1) * 64], in_=in_, identity=C.ident[64 * hc:64 * hc + 64, 64 * hc:64 * hc + 64]))
                    P.op("pe", fns, reads=["hy_kT", "ident"], writes=[("bank", bi)])
                    P.op("act", lambda e, nb8=nb8, bk=bk: e.copy(out=Df[:, nb8 * 8:(nb8 + 1) * 8, :], in_=bk[0:64, 0:512].rearrange("p (a b) -> p a b", b=64)),
                         reads=[("bank", bi)], writes=["hy_Df"])
                for ch in range(64):
                    bi, bk = C.bank()
                    P.op("pe", lambda e, ch=ch, bk=bk: e.matmul(bk[0:64, 0:256], lhsT=Df[:, :, ch], rhs=Rm[0:64, :], start=True, stop=True),
                         reads=["hy_Df", "hy_w"], writes=[("bank", bi)])
                    eng = "act" if ev[0] % 2 == 0 else "dve"
                    ev[0] += 1
                    src = bk[0:64, 0:256].rearrange("p (a b) -> p a b", b=128)
                    if eng == "act":
                        P.op("act", lambda e, ch=ch, dr=dr, src=src: e.copy(out=Y[dr][:, :, :, ch], in_=src), reads=[("bank", bi)], writes=[("hy_Y", dr)])
                    else:
                        P.op("dve", lambda e, ch=ch, dr=dr, src=src: e.tensor_copy(out=Y[dr][:, :, :, ch], in_=src), reads=[("bank", bi)], writes=[("hy_Y", dr)])
            for k2c in range(16):
                vi = k2c % 2
                P.dma("sp", Vt[vi][:, :, :, :], Dm["hy_V"][:, k2c * 8:(k2c + 1) * 8, :, :], writes=[("hy_Vt", vi)])
                bir, bkr = C.bank()
                bii, bki = C.bank()
                fr, fi = [], []
                for k2l in range(8):
                    k2 = k2c * 8 + k2l
                    cs = slice(k2l * 64, (k2l + 1) * 64)
                    for n_, (va, yy, hh) in enumerate([(2, 0, 0), (3, 0, 1), (2, 1, 0), (3, 1, 1)]):
                        fr.append(lambda e, va=va, yy=yy, hh=hh, k2=k2, k2l=k2l, cs=cs, n_=n_, vi=vi, bkr=bkr: e.matmul(bkr[:, cs], lhsT=Vt[vi][:, k2l, va, :], rhs=Y[yy][:, hh, k2, :], start=(n_ == 0), stop=(n_ == 3)))
                    for n_, (va, yy, hh) in enumerate([(4, 0, 0), (2, 0, 1), (3, 1, 0), (5, 1, 1)]):
                        fi.append(lambda e, va=va, yy=yy, hh=hh, k2=k2, k2l=k2l, cs=cs, n_=n_, vi=vi, bki=bki: e.matmul(bki[:, cs], lhsT=Vt[vi][:, k2l, va, :], rhs=Y[yy][:, hh, k2, :], start=(n_ == 0), stop=(n_ == 3)))
                P.op("pe", fr, reads=[("hy_Vt", vi), ("hy_Y", 0), ("hy_Y", 1)], writes=[("bank", bir)])
                P.op("pe", fi, reads=[("hy_Vt", vi), ("hy_Y", 0), ("hy_Y", 1)], writes=[("bank", bii)])
                ks = kst[k2c % 2]
                P.op("act", lambda e, ks=ks, bkr=bkr: e.copy(out=ks[:, 0, :, :], in_=bkr[:, 0:512].rearrange("p (a b) -> p a b", b=64)), reads=[("bank", bir)], writes=[("hy_kst", k2c % 2)])
                P.op("dve", lambda e, ks=ks, bki=bki: e.tensor_copy(out=ks[:, 1, :, :], in_=bki[:, 0:512].rearrange("p (a b) -> p a b", b=64)), reads=[("bank", bii), ("hy_kst", k2c % 2)], writes=[("hy_kst", k2c % 2)])
                P.dma("sp", Kscr[o][hc][:, :, k2c * 8:(k2c + 1) * 8, :], ks[:, :, :, :], reads=[("hy_kst", k2c % 2)], writes=[("Kscr", o, hc)])


def emit_shortconv(C, src, dst, w, b, rkeys, wkeys, n=512):
    P = C.P
    s3 = src.rearrange("p (a b) -> p a b", b=64)
    d3 = dst.rearrange("p (a b) -> p a b", b=64)
    P.op("dve", lambda e: e.tensor_scalar(out=dst, in0=src, scalar1=w[:, 1:2], scalar2=b, op0=ALU.mult, op1=ALU.add), reads=rkeys, writes=wkeys)
    P.op("dve", lambda e: e.scalar_tensor_tensor(out=d3[:, :, 1:64], in0=s3[:, :, 0:63], scalar=w[:, 0:1], in1=d3[:, :, 1:64], op0=ALU.mult, op1=ALU.add), reads=rkeys + wkeys, writes=wkeys)
    P.op("dve", lambda e: e.scalar_tensor_tensor(out=d3[:, :, 0:63], in0=s3[:, :, 1:64], scalar=w[:, 2:3], in1=d3[:, :, 0:63], op0=ALU.mult, op1=ALU.add), reads=rkeys + wkeys, writes=wkeys)


def hy_alloc_data(C):
    H = {}
    H["Z"] = C.sb("hy_Z", [128, 64, 128])
    H["G"] = [C.sb("hy_G%d" % i, [128, 64, 128], BF16) for i in range(2)]
    H["mark"] = C.ptr
    H["xT3"] = [C.sb("hy_x%d" % i, [128, 2, HL]) for i in range(3)]
    return H


def hy_conv(C, Dm, Kscr, H, yhyT, skip_layout=False):
    P = C.P
    Z, G = H["Z"], H["G"]
    for which in (() if skip_layout else range(3)):
        src = H["xT3"][which]
        for hc in range(2):
            base = src[64 * hc:64 * hc + 64, :, :]
            pst = list(base.ap[0])
            for nb8 in range(8):
                bi, bk = C.bank()
                fns = []
                for j in range(8):
                    n1 = nb8 * 8 + j
                    in_ = mkap(base, n1, [pst, [HL, 2], [64, 64]])
                    fns.append(lambda e, j=j, in_=in_, bk=bk, hc=hc: e.transpose(out=bk[:, j * 64:(j + 1) * 64], in_=in_, identity=C.ident[64 * hc:64 * hc + 64, 64 * hc:64 * hc + 64]))
                P.op("pe", fns, reads=[("hy_x", which), "ident"], writes=[("bank", bi)])
                dst = (Z if which == 0 else G[which - 1])[:, nb8 * 8:(nb8 + 1) * 8, 64 * hc:64 * hc + 64]
                srcv = bk[:, 0:512].rearrange("p (a b) -> p a b", b=64)
                if (nb8 + which) % 2 == 0:
                    P.op("act", lambda e, dst=dst, srcv=srcv: e.copy(out=dst, in_=srcv), reads=[("bank", bi)], writes=[("hy_Z", which, hc)])
                else:
                    P.op("dve", lambda e, dst=dst, srcv=srcv: e.tensor_copy(out=dst, in_=srcv), reads=[("bank", bi)], writes=[("hy_Z", which, hc)])
    C.P.barrier()
    C.ptr = H["mark"]
    Rm = C.sb("hyc_Rm", [128, 256])
    A12 = C.sb("hyc_A", [128, 2, 128], BF16)
    biasr = C.sb("hyc_bias", [128, 2, 128])
    P.dma("sp", Rm[:, :], Dm["hy_Rm"][:, :], writes=["hyc_c"])
    P.dma("sp", A12[:, :, :], Dm["hy_A"][:, :, :], writes=["hyc_c"])
    P.dma("sp", biasr[:, :, :], Dm["hy_biasr"][:, :, :], writes=["hyc_c"])
    Y = C.sb("hyc_Y", [64, 2, 128, 64], BF16)
    Tt = Y
    S = C.sb("hyc_S", [128, 128, 64], BF16)
    Q2 = C.sb("hyc_Q2", [128, 128, 64], BF16)
    Yfull = mkap(Y, 0, [list(S[:, :, :].ap[0]), [1, 2 * 128 * 64]])
    Tt = Yfull[:, 0:8192].rearrange("p (c n h) -> p c n h", c=2, n=64)
    Q1 = Yfull[:, 8192:16384].rearrange("p (k h) -> p k h", h=64)
    Vt = [C.sb("hyc_Vt%d" % i, [64, 8, 2, 128], BF16) for i in range(2)]
    Bt = [C.sb("hyc_Bt%d" % i, [128, 8, 2, 128], BF16) for i in range(2)]
    Kt = [C.sb("hyc_Kt%d" % i, [128, 2, 8, 64]) for i in range(2)]
    gt = [C.sb("hyc_gt%d" % i, [128, 8, 64]) for i in range(2)]
    ev = [0]

    def evac(out, in_, bi, wkeys, rkeys=()):
        eng = "act" if ev[0] % 2 == 0 else "dve"
        ev[0] += 1
        if eng == "act":
            P.op("act", lambda e: e.copy(out=out, in_=in_), reads=[("bank", bi)] + list(rkeys), writes=wkeys)
        else:
            P.op("dve", lambda e: e.tensor_copy(out=out, in_=in_), reads=[("bank", bi)] + list(rkeys), writes=wkeys)

    for hc in range(2):
        hs = slice(64 * hc, 64 * hc + 64)
        for o in range(2):
            for ch in range(64):
                bi, bk = C.bank()
                P.op("pe", lambda e, ch=ch, bk=bk, hc=hc: e.matmul(bk[0:64, 0:256], lhsT=Z[:, :, 64 * hc + ch], rhs=Rm[:, :], start=True, stop=True),
                     reads=[("hy_Z", 0, hc), "hyc_c"], writes=[("bank", bi)])
                evac(Y[:, :, :, ch], bk[0:64, 0:256].rearrange("p (a b) -> p a b", b=128), bi, ["hyc_Yreg", "hyc_T"])
            for k2c in range(16):
                vi = k2c % 2
                P.dma("sp", Vt[vi][:, :, :, :], Dm["hy_V"][:, k2c * 8:(k2c + 1) * 8, 0:2, :], writes=[("hyc_Vt", vi)])
                bi, bk = C.bank()
                fns = []
                for k2l in range(8):
                    k2 = k2c * 8 + k2l
                    cs = slice(k2l * 64, (k2l + 1) * 64)
                    fns.append(lambda e, k2=k2, k2l=k2l, cs=cs, vi=vi, bk=bk: e.matmul(bk[:, cs], lhsT=Vt[vi][:, k2l, 0, :], rhs=Y[:, 0, k2, :], start=True, stop=False))
                    fns.append(lambda e, k2=k2, k2l=k2l, cs=cs, vi=vi, bk=bk: e.matmul(bk[:, cs], lhsT=Vt[vi][:, k2l, 1, :], rhs=Y[:, 1, k2, :], start=False, stop=True))
                P.op("pe", fns, reads=[("hyc_Vt", vi), "hyc_Yreg"], writes=[("bank", bi)])
                evac(S[:, k2c * 8:(k2c + 1) * 8, :], bk[:, 0:512].rearrange("p (a b) -> p a b", b=64), bi, ["hyc_S"])
            for k2c in range(16):
                ki_ = k2c % 2
                P.dma("sp", Kt[ki_][:, :, :, :], Kscr[o][hc][:, :, k2c * 8:(k2c + 1) * 8, :], reads=[("Kscr", o, hc)], writes=[("hyc_Kt", ki_)])
                ks = slice(k2c * 8, (k2c + 1) * 8)
                P.op("dve", lambda e, ks=ks, ki_=ki_: e.tensor_tensor(out=Q1[:, ks, :], in0=S[:, ks, :], in1=Kt[ki_][:, 0, :, :], op=ALU.mult),
                     reads=["hyc_S", ("hyc_Kt", ki_)], writes=["hyc_Yreg"])
                P.op("pool", lambda e, ks=ks, ki_=ki_: e.tensor_tensor(out=Q2[:, ks, :], in0=S[:, ks, :], in1=Kt[ki_][:, 1, :, :], op=ALU.mult),
                     reads=["hyc_S", ("hyc_Kt", ki_)], writes=["hyc_Q2"])
            for c4 in range(16):
                bi, bk = C.bank()
                fns = []
                for j in range(4):
                    ch = c4 * 4 + j
                    cs = slice(j * 128, (j + 1) * 128)
                    fns.append(lambda e, ch=ch, cs=cs, bk=bk: e.matmul(bk[:, cs], lhsT=Q1[:, :, ch], rhs=A12[:, 0, :], start=True, stop=False))
                    fns.append(lambda e, ch=ch, cs=cs, bk=bk: e.matmul(bk[:, cs], lhsT=Q2[:, :, ch], rhs=A12[:, 1, :], start=False, stop=True))
                P.op("pe", fns, reads=["hyc_Yreg", "hyc_Q2", "hyc_c"], writes=[("bank", bi)])
                for j in range(4):
                    ch = c4 * 4 + j
                    evac(Tt[:, :, :, ch], bk[:, j * 128:(j + 1) * 128].rearrange("p (a b) -> p a b", b=64), bi, ["hyc_T"])
            for nc8 in range(8):
                bt = nc8 % 2
                P.dma("sp", Bt[bt][:, :, :, :], Dm["hy_Bf"][:, nc8 * 8:(nc8 + 1) * 8, :, :], writes=[("hyc_Bt", bt)])
                bi, bk = C.bank()
                fns = []
                for nl in range(8):
                    n1 = nc8 * 8 + nl
                    cs = slice(nl * 64, (nl + 1) * 64)
                    fns.append(lambda e, n1=n1, nl=nl, cs=cs, bt=bt, bk=bk: e.matmul(bk[:, cs], lhsT=Bt[bt][:, nl, 0, :], rhs=Tt[:, 0, n1, :], start=True, stop=False))
                    fns.append(lambda e, n1=n1, nl=nl, cs=cs, bt=bt, bk=bk: e.matmul(bk[:, cs], lhsT=Bt[bt][:, nl, 1, :], rhs=Tt[:, 1, n1, :], start=False, stop=True))
                P.op("pe", fns, reads=[("hyc_Bt", bt), "hyc_T"], writes=[("bank", bi)])
                ns = slice(nc8 * 8, (nc8 + 1) * 8)
                g = gt[nc8 % 2]
                Zs = Z[:, ns, hs]
                bb = biasr[:, o, hs]
                bbv = mkap(bb, 0, [list(bb.ap[0]), [0, 8], [1, 64]])
                P.op("pool", lambda e, g=g, Zs=Zs, bbv=bbv: e.tensor_tensor(out=g[:, :, :], in0=Zs, in1=bbv, op=ALU.mult),
                     reads=[("hy_Z", 0, hc), "hyc_c"], writes=[("hyc_gt", nc8 % 2)])
                P.op("dve", lambda e, g=g, bk=bk: e.tensor_tensor(out=g[:, :, :], in0=g[:, :, :], in1=bk[:, 0:512].rearrange("p (a b) -> p a b", b=64), op=ALU.add),
                     reads=[("bank", bi), ("hyc_gt", nc8 % 2)], writes=[("hyc_gt", nc8 % 2)])
                P.op("dve", lambda e, g=g, Zs=Zs, o=o, ns=ns, hs=hs: e.tensor_tensor(out=Zs, in0=g[:, :, :], in1=G[o][:, ns, hs], op=ALU.mult),
                     reads=[("hyc_gt", nc8 % 2), ("hy_Z", o + 1, hc)], writes=[("hy_Z", 0, hc)])
    pst = list(yhyT[:, :, :].ap[0])
    for n4 in range(16):
        bi, bk = C.bank()
        fns = [lambda e, j=j, n4=n4, bk=bk: e.transpose(out=bk[:, j * 128:(j + 1) * 128], in_=Z[:, n4 * 4 + j, :], identity=C.ident[:, :]) for j in range(4)]
        P.op("pe", fns, reads=[("hy_Z", 0, 0), ("hy_Z", 0, 1), "ident"], writes=[("bank", bi)])
        for j in range(4):
            n1 = n4 * 4 + j
            out = mkap(yhyT, n1, [pst, [HL, 2], [64, 64]])
            evac(out, bk[:, j * 128:(j + 1) * 128].rearrange("p (a b) -> p a b", b=64), bi, ["yhyT"])


def hy_host_inputs(inp, j):
    from_ch = slice(128 * j, 128 * j + 128)
    m = {}
    m["hy_w1"] = np.ascontiguousarray(inp["hy_w1"][0])
    m["hy_w2"] = np.ascontiguousarray(inp["hy_w2"][0])
    m["hy_w3"] = np.ascontiguousarray(inp["hy_w3"][0])
    m["hy_col"] = np.ascontiguousarray(np.stack([inp["hy_freq"][0], inp["hy_b1"][0], inp["hy_b2"][0], inp["hy_b3"][0]], 1))
    wo = inp["hy_w_out"][0].reshape(64, 2, 2, 1024)[:, :, :, from_ch].reshape(64, 4, 128)
    m["hy_wout"] = np.ascontiguousarray(wo)
    m["hy_biasr"] = np.ascontiguousarray(np.broadcast_to(inp["hy_bias"][0][:, from_ch][None], (128, 2, 128))).astype(np.float32)
    return m


def hy_const_inputs(j):
    import math
    L = HL
    t = np.linspace(0.0, 1.0, L, dtype=np.float32)[:, None]
    w = (np.float32(2.0 * math.pi / L) * np.arange(L, dtype=np.float32))[:, None]
    bands = np.linspace(1e-4, 15, 16, dtype=np.float32)[None, :]
    feats = np.concatenate([t, np.cos(bands * w), -np.sin(bands * w)], -1).astype(np.float32)
    deltas = np.abs(np.linspace(math.log(1e-2) / 1.5, math.log(1e-2) / 0.3, 4096, dtype=np.float32))
    dsel = deltas.reshape(2, 2, 1024)[:, :, 128 * j:128 * j + 128].reshape(4, 128).T
    nd = (-dsel / np.float32(L - 1)).astype(np.float32)
    nb = np.stack([nd * np.float32(c * 512 - 1) for c in range(NCHK)], -1).astype(np.float32)
    Rm, V, A1, A2, Bf = _hy_dft_constants()
    bf = ml_dtypes.bfloat16
    return {"hy_feats": np.ascontiguousarray(feats.T), "hy_nd": np.ascontiguousarray(nd), "hy_nb": np.ascontiguousarray(nb),
            "hy_Rm": Rm.astype(np.float32), "hy_V": V.astype(bf), "hy_A": np.ascontiguousarray(np.stack([A1, A2], 1)).astype(bf),
            "hy_Bf": Bf.astype(bf),
            "tau": np.broadcast_to(np.arange(1, 513, dtype=np.float32)[None], (128, 512)).copy(),
            "ident": np.eye(128, dtype=np.float32)}


_HYC = {}


def _hy_dft_constants():
    if "c" in _HYC:
        return _HYC["c"]
    N = 8192
    r = np.arange(64)[:, None]; k2 = np.arange(128)[None, :]
    chi = 2 * np.pi * r * k2 / 128
    Fre, Fim = np.cos(chi), -np.sin(chi)
    Rm = np.zeros((128, 256))
    Rm[0:64, 0:128] = Fre; Rm[64:128, 0:128] = -Fim
    Rm[0:64, 128:256] = Fim; Rm[64:128, 128:256] = Fre
    n1 = np.arange(64)[:, None, None]; k2 = np.arange(128)[None, :, None]; k1 = np.arange(64)[None, None, :]
    psi = 2 * np.pi * (n1 * k1 / 64 + n1 * k2 / 8192)
    Gre, Gim = np.cos(psi), -np.sin(psi)
    V = np.zeros((64, 128, 6, 128))
    V[:, :, 0, :64] = Gre; V[:, :, 0, 64:] = Gim
    V[:, :, 1, :64] = -Gim; V[:, :, 1, 64:] = Gre
    V[:, :, 2, :64] = Gre; V[:, :, 2, 64:] = Gre
    V[:, :, 3, :64] = -Gim; V[:, :, 3, 64:] = -Gim
    V[:, :, 4, :64] = Gim; V[:, :, 4, 64:] = Gim
    V[:, :, 5, :64] = -Gre; V[:, :, 5, 64:] = -Gre
    k1 = np.arange(64)[:, None]; n1 = np.arange(64)[None, :]
    om = 2 * np.pi * n1 * k1 / 64
    c, s = np.cos(om), np.sin(om)
    A1 = np.zeros((128, 128)); A2 = np.zeros((128, 128))
    A1[:64, :64] = c; A1[64:, :64] = -s; A1[:64, 64:] = s; A1[64:, 64:] = c
    A2[:64, :64] = -s; A2[64:, :64] = -c; A2[:64, 64:] = c; A2[64:, 64:] = -s
    k2 = np.arange(128)[:, None, None]; n1 = np.arange(64)[None, :, None]; r = np.arange(64)[None, None, :]
    phi = 2 * np.pi * (r * k2 / 128 + n1 * k2 / 8192)
    cp, sp = np.cos(phi) / N, np.sin(phi) / N
    Bf = np.zeros((128, 64, 2, 128))
    Bf[:, :, 0, :64] = cp; Bf[:, :, 1, :64] = -sp
    Bf[:, :, 0, 64:] = sp; Bf[:, :, 1, 64:] = cp
    _HYC["c"] = (Rm, V, A1, A2, Bf)
    return _HYC["c"]


def _custom(P, q, fn, reads, writes, skey, inc=1):
    if P.mute:
        return
    need = P._need(reads, writes)
    P._emit_waits(q, need)
    sem = P._dsem(skey)
    P.dcnt[skey] += inc
    val = P.dcnt[skey]
    P.q[q].append(lambda e, fn=fn, sem=sem, inc=inc: fn(e).then_inc(sem, inc))
    P._update((sem, val, "d:" + str(skey)), reads, writes)


def load_cast(C, view, dst, n_a, n_b, key, rkeys=()):
    P = C.P
    per = max(1, 2048 // n_b)
    a0 = 0
    while a0 < n_a:
        a1 = min(n_a, a0 + per)
        i = C.wcnt % 2
        C.wcnt += 1
        st = C.wst[i].rearrange("p a b -> p (a b)")[:, 0:(a1 - a0) * n_b].rearrange("p (a b) -> p a b", b=n_b)
        P.dma("sp", st, view[:, a0:a1, :], reads=list(rkeys), writes=[("wst", i)])
        P.op("pool", lambda e, st=st, a0=a0, a1=a1: e.tensor_copy(out=dst[:, a0:a1, :], in_=st), reads=[("wst", i)], writes=[key])
        a0 = a1


MODE_INPUTS = {
    "LA": {"x_tok", "scin", "w_ada", "bT", "gT", "wg0", "wu0", "wd0", "ident"},
    "LB": {"ident", "w_mix", "s5pp", "s5row", "s5braw", "s5craw", "s5dsk", "s5tau", "hy_feats", "hy_w1", "hy_w2", "hy_w3", "hy_col",
           "hy_wout", "hy_nd", "hy_nb", "tau", "hy_Rm", "hy_V", "hy_A", "hy_Bf", "hy_biasr", "scw", "scb"},
    "LM": {"w_gt", "w_pa_s", "w_pb_s"},
}


def build_fused(stop=None, mode=None):
    nc = bass.Bass("TRN2", target_bir_lowering=False)
    Dm = {}

    def din(name, shape, dt=F32):
        if mode is None or name in MODE_INPUTS[mode]:
            Dm[name] = nc.dram_tensor(name, shape, dt, kind="ExternalInput")
        else:
            Dm[name] = nc.dram_tensor("unused_" + name, shape, dt)
        return Dm[name]

    x_tok = din("x_tok", [T, D]); scin = din("scin", [128, KT, 2]); w_ada = din("w_ada", [D, 9 * D])
    bT_d = din("bT", [128, 9, KT]); gT_d = din("gT", [128, 4, KT])
    wg0 = din("wg0", [D, DFF]); wu0 = din("wu0", [D, DFF]); wd0 = din("wd0", [DFF, D])
    if stop != "L1":
        wg1 = din("wg1", [D, DFF]); wu1 = din("wu1", [D, DFF]); wd1 = din("wd1", [DFF, D])
    din("ident", [128, 128])
    w_mix = din("w_mix", [D, 4, 128]); w_gt = din("w_gt", [D, 2, 256])
    w_pa_s = din("w_pa_s", [1024, 2, 256]); w_pb_s = din("w_pb_s", [1024, 256])
    if stop != "L1":
        w_out_s = din("w_out_s", [256, D])
    s5pp = din("s5pp", [128, 3, 2, 4]); s5row = din("s5row", [128, 3, 2, 4, 128]); s5braw = din("s5braw", [128, 2, 4, 2, 128])
    s5craw = din("s5craw", [128, 2, 4, 2, 128]); s5dsk = din("s5dsk", [128, 1]); s5tau = din("s5tau", [128, LC])
    din("hy_feats", [33, HL]); din("hy_w1", [33, 64]); din("hy_w2", [64, 64]); din("hy_w3", [64, 64]); din("hy_col", [64, 4])
    din("hy_wout", [64, 4, 128]); din("hy_nd", [128, 4]); din("hy_nb", [128, 4, NCHK]); din("tau", [128, 512]); din("hy_Rm", [128, 256])
    din("hy_V", [64, 128, 6, 128], BF16); din("hy_A", [128, 2, 128], BF16); din("hy_Bf", [128, 64, 2, 128], BF16)
    din("hy_biasr", [128, 2, 128]); scw_d = din("scw", [128, 3, 3]); scb_d = din("scb", [128, 3])
    two = (stop == "L1") or (mode is not None)
    if two:
        if mode in (None, "LM"):
            mT_o = nc.dram_tensor("mT_o", [2, 128, 2 * HL], BF16, kind="ExternalOutput")
        else:
            mT_o = nc.dram_tensor("mT_unused", [2, 128, 2 * HL], BF16)
        if mode in (None, "LA"):
            xT_o = nc.dram_tensor("xT_o", [128, KT, T], F32, kind="ExternalOutput")
            modT_o = nc.dram_tensor("modT_o", [128, 9, KT, 2], F32, kind="ExternalOutput")
        else:
            xT_o = nc.dram_tensor("xT_unused", [128, KT, T], F32)
            modT_o = nc.dram_tensor("modT_unused", [128, 9, KT, 2], F32)
    else:
        out_d = nc.dram_tensor("out", [TX, D], F32, kind="ExternalOutput")
    Kscr = [[nc.dram_tensor("Kscr%d%d" % (o, hc), [128, 2, 128, 64], F32) for hc in range(2)] for o in range(2)]
    agi = nc.dram_tensor("agi", [128, KT * T], BF16, **({"kind": "ExternalOutput"} if mode == "LA" else {}))
    ago = nc.dram_tensor("ago", [128 * NCORES, KT * T], BF16, **({"kind": "ExternalInput"} if mode in ("LB", "LM") else {}))
    xspill = nc.dram_tensor("xspill", [128, KT, T], F32)
    ag2i = nc.dram_tensor("ag2i", [256, 2 * HL], BF16, **({"kind": "ExternalOutput"} if mode == "LB" else {}))
    ag2o = nc.dram_tensor("ag2o", [256 * NCORES, 2 * HL], BF16, **({"kind": "ExternalInput"} if mode == "LM" else {}))
    rsi = nc.dram_tensor("rsi", [NCORES * D, TX], BF16)
    rso = nc.dram_tensor("rso", [D, TX], BF16)
    RG = [list(range(NCORES))]
    with ExitStack() as ctx:
        C = Ctx(nc, ctx)
        P = C.P
        C.wcnt = 0
        alloc_persist(C)
        modT = C.sb("modT", [128, 9, KT, 2], persist=True)
        gT = C.sb("gTs", [128, 4, KT], persist=True)
        P.dma("sp", C.ident[:, :], Dm["ident"][:, :], writes=["ident"])
        P.mute = mode in ("LA", "LM")
        hy_filters(C, Dm, Kscr)
        P.mute = mode in ("LB", "LM")
        if stop == "0":
            P.barrier()
            P.replay()
            return nc
        C.phase()
        alloc_common(C)
        sc = C.sb("sc", [128, KT, 2])
        bT = C.sb("bTs", [128, 9, KT])
        P.dma("sp", sc[:, :, :], scin[:, :, :], writes=["sc"])
        P.dma("sp", bT[:, :, :], bT_d[:, :, :], writes=["bT"])
        P.dma("sp", gT[:, :, :], gT_d[:, :, :], writes=["gT"])
        P.op("act", lambda e: e.activation(out=sc[:, :, :], in_=sc[:, :, :], func=AF.Silu), reads=["sc"], writes=["sc"])
        wa_v = w_ada.ap().rearrange("(kt p) c -> p kt c", p=128)
        for cb in range(9 * D // 128):
            i = cb % 2
            st = C.wst[i]
            P.dma("sp", st[:, :, :], wa_v[:, :, cb * 128:(cb + 1) * 128], writes=[("wst", i)])
            bi, bk = C.bank()
            P.op("pe", [lambda e, kt=kt, st=st, bk=bk: e.matmul(bk[:, 0:2], lhsT=st[:, kt, :], rhs=sc[:, kt, :], start=(kt == 0), stop=(kt == KT - 1)) for kt in range(KT)],
                 reads=[("wst", i), "sc"], writes=[("bank", bi)])
            blk, k2 = cb // KT, cb % KT
            P.op("dve", lambda e, bk=bk, blk=blk, k2=k2: e.tensor_scalar(out=modT[:, blk, k2, :], in0=bk[:, 0:2], scalar1=bT[:, blk, k2:k2 + 1], scalar2=None, op0=ALU.add),
                 reads=[("bank", bi), "bT"], writes=["modT"])
        xs = [C.sb("xs%d" % i, [128, D]) for i in range(2)]
        for tt in range(9):
            nt = 128 if tt < 8 else 64
            i = tt % 2
            P.dma("sp", xs[i][0:nt, :], x_tok[tt * 128:tt * 128 + nt, :], writes=[("xs", i)])
            for k4 in range(4):
                bi, bk = C.bank()
                P.op("pe", [lambda e, j=j, k4=k4, bk=bk, i=i, nt=nt: e.transpose(out=bk[:, j * 128:j * 128 + nt], in_=xs[i][0:nt, (k4 * 4 + j) * 128:(k4 * 4 + j + 1) * 128], identity=C.ident[0:nt, 0:nt]) for j in range(4)],
                     reads=[("xs", i), "ident"], writes=[("bank", bi)])
                for j in range(4):
                    kt = k4 * 4 + j
                    P.op("act", lambda e, j=j, kt=kt, bk=bk, nt=nt, tt=tt: e.copy(out=C.xT[:, kt, tt * 128:tt * 128 + nt], in_=bk[:, j * 128:j * 128 + nt]),
                         reads=[("bank", bi)], writes=[("xT", kt)])
        gp, sh, hg = emit_mods(C, modT, gT, 0, "s0")
        emit_norm_mod(C, C.xT, C.hT, None, C.rstd, C.onesb, gp, sh, "s0")
        emit_ffn(C, wg0, wu0, wd0, C.hT, C.xT, hg, "s0")
        gp1, sh1, _ = emit_mods(C, modT, gT, 1, "s1")
        emit_norm_mod(C, C.xT, C.hT, None, C.rstd, C.onesb, gp1, sh1, "s1")
        allx = [("xT", kt) for kt in range(KT)]
        allh = [("hT", kt) for kt in range(KT)]
        P.dma("sp", xspill[:, :, :], C.xT[:, :, :], reads=allx, writes=["xspill"])
        if two:
            P.dma("sp", xT_o[:, :, :], C.xT[:, :, :], reads=allx, writes=["xT_o"])
            P.dma("sp", modT_o[:, :, :, :], modT[:, :, :, :], reads=["modT"], writes=["modT_o"])
        P.dma("pool", agi.ap().rearrange("p (k t) -> p k t", t=T), C.hT[:, :, :], reads=allh, writes=["agi"])
        if mode is None:
            _custom(P, "pool", lambda e: e.collective_compute("AllGather", ALU.bypass, replica_groups=RG, ins=[agi.ap().opt()], outs=[ago.ap().opt()]),
                    ["agi"], ["ago"], "ago")
        if mode == "LA":
            P.barrier()
            P.replay()
            return nc
        P.mute = mode in ("LA", "LM")
        ago_v = ago.ap().rearrange("(r p) (k t) -> r p k t", p=128, t=T)
        if stop == "A":
            P.barrier()
            P.replay()
            return nc
        C.phase()
        uT = C.sb("uT", [128, 2, CTXL + SEQ], BF16)
        yhyT = C.sb("yhyT", [128, 2, HL], BF16)
        markA = C.ptr
        H = {}
        H["Z"] = C.sb("hy_Z", [128, 64, 128])
        H["G"] = [C.sb("hy_G%d" % i, [128, 64, 128], BF16) for i in range(2)]
        H["mark"] = C.ptr
        xsA = C.sb("hy_xA", [128, 2, HL])
        H["xT3"] = [xsA, xsA, xsA]
        C.wst = [C.sb("wstB%d" % i, [128, KT, 128]) for i in range(2)]
        win = C.sb("win", [128, KT, 4 * 128], BF16)
        scw = C.sb("scw", [128, 3, 3]); scb = C.sb("scb", [128, 3])
        hch = [C.sb("hch%d" % i, [128, KT, 512], BF16) for i in range(2)]
        P.dma("sp", scw[:, :, :], scw_d[:, :, :], writes=["scw"])
        P.dma("sp", scb[:, :], scb_d[:, :], writes=["scw"])
        load_cast(C, w_mix.ap().rearrange("(kt p) a c -> p kt (a c)", p=128), win, KT, 512, "win")
        hcnt = [0]

        def proj_pass(blocks, with_ctx):
            for r in range(NCORES):
                b = r // 4
                for (a, e_) in CH:
                    is_ctx = a >= TX
                    if is_ctx and not with_ctx:
                        continue
                    n = e_ - a
                    i = hcnt[0] % 2
                    hcnt[0] += 1
                    P.dma("sp", hch[i][:, :, 0:n], ago_v[r, :, :, a:e_], reads=["ago"], writes=[("hch", i)])
                    for (cblk, sink, lat_only) in blocks:
                        if is_ctx and lat_only:
                            continue
                        bi, bk = C.bank()
                        P.op("pe", [lambda e, kt=kt, bk=bk, i=i, n=n, cblk=cblk: e.matmul(bk[:, 0:n], lhsT=win[:, kt, cblk * 128:(cblk + 1) * 128], rhs=hch[i][:, kt, 0:n], start=(kt == 0), stop=(kt == KT - 1)) for kt in range(KT)],
                             reads=["win", ("hch", i)], writes=[("bank", bi)])
                        sink(bk, bi, n, b, (r % 4), a, is_ctx)

        def sink_u(bk, bi, n, b, rq, a, is_ctx):
            pos = (rq * TC + (a - TX)) if is_ctx else (CTXL + rq * TX + a)
            P.op("act", lambda e: e.copy(out=uT[:, b, pos:pos + n], in_=bk[:, 0:n]), reads=[("bank", bi)], writes=[("uT", b)])

        def mk_sink_conv(which):
            def sink(bk, bi, n, b, rq, a, is_ctx):
                pos = rq * TX + a
                emit_shortconv(C, bk[:, 0:n], H["xT3"][which][:, b, pos:pos + n], scw[:, which, :], scb[:, which:which + 1], [("bank", bi), "scw"], [("hy_x", 0)], n=n)
            return sink

        def to_layout(which):
            src = H["xT3"][which]
            for hc in range(2):
                base = src[64 * hc:64 * hc + 64, :, :]
                pst = list(base.ap[0])
                for nb8 in range(8):
                    bi, bk = C.bank()
                    fns = []
                    for j in range(8):
                        n1 = nb8 * 8 + j
                        in_ = mkap(base, n1, [pst, [HL, 2], [64, 64]])
                        fns.append(lambda e, j=j, in_=in_, bk=bk, hc=hc: e.transpose(out=bk[:, j * 64:(j + 1) * 64], in_=in_, identity=C.ident[64 * hc:64 * hc + 64, 64 * hc:64 * hc + 64]))
                    P.op("pe", fns, reads=[("hy_x", 0), "ident"], writes=[("bank", bi)])
                    dst = (H["Z"] if which == 0 else H["G"][which - 1])[:, nb8 * 8:(nb8 + 1) * 8, 64 * hc:64 * hc + 64]
                    srcv = bk[:, 0:512].rearrange("p (a b) -> p a b", b=64)
                    P.op("act" if nb8 % 2 == 0 else "dve", (lambda e, dst=dst, srcv=srcv: e.copy(out=dst, in_=srcv)) if nb8 % 2 == 0 else (lambda e, dst=dst, srcv=srcv: e.tensor_copy(out=dst, in_=srcv)),
                         reads=[("bank", bi)], writes=[("hy_Z", which, hc)])

        proj_pass([(0, sink_u, False), (1, mk_sink_conv(0), True)], True)
        to_layout(0)
        proj_pass([(2, mk_sink_conv(1), True)], False)
        to_layout(1)
        proj_pass([(3, mk_sink_conv(2), True)], False)
        to_layout(2)
        if stop == "B1":
            P.barrier()
            P.replay()
            return nc
        hy_conv(C, Dm, Kscr, H, yhyT, skip_layout=True)
        P.barrier()
        C.ptr = markA
        ys5T = C.sb("ys5T", [128, 2, SEQ], BF16)
        S = s5_setup(C, s5pp, s5row, s5braw, s5craw, s5dsk, s5tau)
        emit_s5(C, S, uT, ys5T)
        a2v = ag2i.ap().rearrange("(w p) (b t) -> w p b t", p=128, t=HL)
        P.dma("pool", a2v[0], ys5T[:, :, :], reads=[("ys5T", 0), ("ys5T", 1)], writes=["ag2i"])
        P.dma("pool", a2v[1], yhyT[:, :, :], reads=["yhyT"], writes=["ag2i"])
        if mode is None:
            _custom(P, "pool", lambda e: e.collective_compute("AllGather", ALU.bypass, replica_groups=RG, ins=[ag2i.ap().opt()], outs=[ag2o.ap().opt()]),
                    ["ag2i"], ["ag2o"], "ag2o")
        if mode == "LB":
            P.barrier()
            P.replay()
            return nc
        P.mute = False
        if stop == "B":
            P.barrier()
            P.replay()
            return nc
        C.phase()
        C.wst = [C.sb("wstM%d" % i, [128, KT, 128]) for i in range(2)]
        wpa = C.sb("wpa", [128, 8, 512], BF16)
        wpb = C.sb("wpb", [128, 8, 256], BF16)
        wgt = C.sb("wgt", [128, KT, 512], BF16)
        wo = C.sb("wo", [128, 2, D], BF16)
        load_cast(C, w_pa_s.ap().rearrange("(kt p) a c -> p kt (a c)", p=128), wpa, 8, 512, "wM")
        load_cast(C, w_pb_s.ap().rearrange("(kt p) c -> p kt c", p=128), wpb, 8, 256, "wM")
        load_cast(C, w_gt.ap().rearrange("(kt p) a c -> p kt (a c)", p=128), wgt, KT, 512, "wM")
        if not two:
            load_cast(C, w_out_s.ap().rearrange("(kt p) c -> p kt c", p=128), wo, 2, D, "wM")
        ysb = [C.sb("ysb%d" % i, [128, 8, 512], BF16) for i in range(2)]
        yhb = [C.sb("yhb%d" % i, [128, 8, 512], BF16) for i in range(2)]
        hcm = [C.sb("hcm%d" % i, [128, KT, 512], BF16) for i in range(2)]
        gsb = C.sb("gsb", [128, 8, 512], BF16)
        g1 = C.sb("gel1", [128, 8, 512])
        g2 = C.sb("gel2", [128, 8, 512])
        mT = C.sb("mT", [128, 2, 512], BF16)
        tA = [C.sb("mtA%d" % i, [128, 512]) for i in range(4)]
        pst_ = [C.sb("pstg%d" % i, [128, KT, 512], BF16) for i in range(2)]
        a2o_v = ag2o.ap().rearrange("(r w p) g -> r w p g", w=2, p=128)
        rsi_v = rsi.ap().rearrange("(i kt p) t -> i p kt t", kt=KT, p=128)
        for c in range(2 * HL // 512):
            i = c % 2
            g0 = c * 512
            r, off = g0 // TX, g0 % TX
            for kt in range(8):
                P.dma("sp", ysb[i][:, kt, :], a2o_v[kt, 0, :, g0:g0 + 512], reads=["ag2o"], writes=[("ysb", i)])
                P.dma("sp", yhb[i][:, kt, :], a2o_v[kt, 1, :, g0:g0 + 512], reads=["ag2o"], writes=[("yhb", i)])
            P.dma("sp", hcm[i][:, :, :], ago_v[r, :, :, off:off + 512], reads=["ago"], writes=[("hcm", i)])
            P.op("pool", lambda e, i=i: e.tensor_tensor(out=g1[:, :, :], in0=ysb[i][:, :, :], in1=ysb[i][:, :, :], op=ALU.mult), reads=[("ysb", i)], writes=["gel1"])
            P.op("dve", lambda e: e.tensor_scalar(out=g1[:, :, :], in0=g1[:, :, :], scalar1=0.044715, scalar2=1.0, op0=ALU.mult, op1=ALU.add), reads=["gel1"], writes=["gel1"])
            P.op("pool", lambda e, i=i: e.tensor_tensor(out=g2[:, :, :], in0=g1[:, :, :], in1=ysb[i][:, :, :], op=ALU.mult), reads=["gel1", ("ysb", i)], writes=["gel2"])
            P.op("act", lambda e: e.activation(out=g2[:, :, :], in_=g2[:, :, :], func=AF.Sigmoid, scale=1.5957691216057308), reads=["gel2"], writes=["gel2"])
            P.op("dve", lambda e, i=i: e.tensor_tensor(out=gsb[:, :, :], in0=g2[:, :, :], in1=ysb[i][:, :, :], op=ALU.mult), reads=["gel2", ("ysb", i)], writes=["gsb"])
            for dt_ in range(2):
                def mmgroup(wt, col0, rhs_t, nk, rkeys):
                    bi, bk = C.bank()
                    P.op("pe", [lambda e, kt=kt, bk=bk: e.matmul(bk[:, 0:512], lhsT=wt[:, kt, col0:col0 + 128], rhs=rhs_t[:, kt, :], start=(kt == 0), stop=(kt == nk - 1)) for kt in range(nk)],
                         reads=["wM"] + rkeys, writes=[("bank", bi)])
                    return bi, bk
                bia, bka = mmgroup(wpa, dt_ * 128, gsb, 8, ["gsb"])
                bib, bkb = mmgroup(wpa, 256 + dt_ * 128, gsb, 8, ["gsb"])
                biy, bky = mmgroup(wpb, dt_ * 128, yhb[i], 8, [("yhb", i)])
                big, bkg = mmgroup(wgt, dt_ * 128, hcm[i], KT, [("hcm", i)])
                bih, bkh = mmgroup(wgt, 256 + dt_ * 128, hcm[i], KT, [("hcm", i)])
                t0_, t1_, t2_, t3_ = [t[:, :] for t in tA]
                P.op("act", lambda e, bkb=bkb, t0_=t0_: e.activation(out=t0_, in_=bkb[:, 0:512], func=AF.Sigmoid), reads=[("bank", bib)], writes=["mtA0"])
                P.op("dve", lambda e, bka=bka, t0_=t0_: e.tensor_tensor(out=t0_, in0=bka[:, 0:512], in1=t0_, op=ALU.mult), reads=[("bank", bia), "mtA0"], writes=["mtA0"])
                P.op("act", lambda e, bkg=bkg, t1_=t1_: e.activation(out=t1_, in_=bkg[:, 0:512], func=AF.Sigmoid), reads=[("bank", big)], writes=["mtA1"])
                P.op("act", lambda e, bkh=bkh, t2_=t2_: e.activation(out=t2_, in_=bkh[:, 0:512], func=AF.Sigmoid), reads=[("bank", bih)], writes=["mtA2"])
                P.op("pool", lambda e, t0_=t0_, t1_=t1_: e.tensor_tensor(out=t0_, in0=t0_, in1=t1_, op=ALU.mult), reads=["mtA0", "mtA1"], writes=["mtA0"])
                P.op("dve", lambda e, bky=bky, t2_=t2_, t3_=t3_: e.tensor_tensor(out=t3_, in0=bky[:, 0:512], in1=t2_, op=ALU.mult), reads=[("bank", biy), "mtA2"], writes=["mtA3"])
                P.op("dve", lambda e, t0_=t0_, t3_=t3_, dt_=dt_: e.tensor_tensor(out=mT[:, dt_, :], in0=t0_, in1=t3_, op=ALU.add), reads=["mtA0", "mtA3"], writes=[("mT", dt_)])
            if two:
                P.dma("pool", mT_o[:, :, g0:g0 + 512].rearrange("a p t -> p a t"), mT[:, :, :], reads=[("mT", 0), ("mT", 1)], writes=["mT_o"])
                continue
            pg = pst_[i]
            for d_ in range(KT):
                bi, bk = C.bank()
                P.op("pe", [lambda e, k=k, d_=d_, bk=bk: e.matmul(bk[:, 0:512], lhsT=wo[:, k, d_ * 128:(d_ + 1) * 128], rhs=mT[:, k, :], start=(k == 0), stop=(k == 1)) for k in range(2)],
                     reads=["wM", ("mT", 0), ("mT", 1)], writes=[("bank", bi)])
                if d_ % 2 == 0:
                    P.op("act", lambda e, bk=bk, d_=d_, pg=pg: e.copy(out=pg[:, d_, :], in_=bk[:, 0:512]), reads=[("bank", bi)], writes=[("pstg", i)])
                else:
                    P.op("dve", lambda e, bk=bk, d_=d_, pg=pg: e.tensor_copy(out=pg[:, d_, :], in_=bk[:, 0:512]), reads=[("bank", bi)], writes=[("pstg", i)])
            P.dma("pool", rsi_v[r, :, :, off:off + 512], pg[:, :, :], reads=[("pstg", i)], writes=["rsi"])
        if two:
            P.wait_all("sp", ["mT_o", "xT_o", "modT_o"])
            P.wait_all("pool", ["mT_o"])
            P.replay()
            print("arena peak words", C.peak, "of", ARENA_WORDS)
            return nc
        _custom(P, "pool", lambda e: e.collective_compute("ReduceScatter", ALU.add, replica_groups=RG, ins=[rsi.ap().opt()], outs=[rso.ap().opt()]),
                ["rsi"], ["rso"], "rso")
        if stop == "M":
            P.barrier()
            P.replay()
            return nc
        C.phase()
        alloc_common(C)
        P.dma("sp", C.xT[:, :, :], xspill[:, :, :], reads=["xspill"], writes=allx)
        mx = [C.sb("mx%d" % i, [128, TX], BF16) for i in range(2)]
        rso_v = rso.ap().rearrange("(kt p) t -> p kt t", p=128)
        for kt in range(KT):
            i = kt % 2
            P.dma("sp", mx[i][:, :], rso_v[:, kt, :], reads=["rso"], writes=[("mx", i)])
            P.op("dve", lambda e, kt=kt, i=i: e.scalar_tensor_tensor(out=C.xT[:, kt, 0:TX], in0=mx[i][:, :], scalar=modT[:, 5, kt, 0:1], in1=C.xT[:, kt, 0:TX], op0=ALU.mult, op1=ALU.add),
                 reads=[("mx", i), "modT"], writes=[("xT", kt)])
        gp2, sh2, hg2 = emit_mods(C, modT, gT, 2, "s2")
        emit_norm_mod(C, C.xT, C.hT, None, C.rstd, C.onesb, gp2, sh2, "s2")
        emit_ffn(C, wg1, wu1, wd1, C.hT, C.xT, hg2, "s2")
        gpf = C.sb("gpf", [128, KT, 2])
        shf = C.sb("shf", [128, KT, 2])
        for r_ in range(2):
            P.op("dve", lambda e, r_=r_: e.tensor_copy(out=gpf[:, :, r_], in_=gT[:, 3, :]), reads=["gT"], writes=["sfgp"])
        P.op("dve", lambda e: e.memset(shf[:, :, :], 0.0), writes=["sfsh"])
        yT = C.xT
        emit_norm_mod(C, C.xT, C.hT, None, C.rstd, C.onesb, gpf, shf, "sf", out32=yT, okey="xT")
        ost = [C.sb("ost%d" % i, [128, D]) for i in range(2)]
        for tt in range(TX // 128):
            i = tt % 2
            for k4 in range(4):
                bi, bk = C.bank()
                P.op("pe", [lambda e, j=j, k4=k4, bk=bk, tt=tt: e.transpose(out=bk[:, j * 128:(j + 1) * 128], in_=yT[:, k4 * 4 + j, tt * 128:(tt + 1) * 128], identity=C.ident[:, :]) for j in range(4)],
                     reads=[("xT", k4 * 4 + j) for j in range(4)] + ["ident"], writes=[("bank", bi)])
                if k4 % 2 == 0:
                    P.op("act", lambda e, bk=bk, k4=k4, i=i: e.copy(out=ost[i][:, k4 * 512:(k4 + 1) * 512], in_=bk[:, 0:512]), reads=[("bank", bi)], writes=[("ost", i)])
                else:
                    P.op("dve", lambda e, bk=bk, k4=k4, i=i: e.tensor_copy(out=ost[i][:, k4 * 512:(k4 + 1) * 512], in_=bk[:, 0:512]), reads=[("bank", bi)], writes=[("ost", i)])
            P.dma("sp", out_d[tt * 128:(tt + 1) * 128, :], ost[i][:, :], reads=[("ost", i)], writes=["out"])
        P.wait_all("sp", ["out"])
        P.replay()
        print("arena peak words", C.peak, "of", ARENA_WORDS)
    return nc


def host_maps(inp):
    f32 = np.float32
    x, c, ctx, c_ctx = inp["x"], inp["c"], inp["ctx"], inp["c_ctx"]
    ctxf = ctx.reshape(2 * CTXL, D)
    w_ada = np.ascontiguousarray(inp["w_ada"][0])
    bT = np.ascontiguousarray(inp["b_ada"][0].reshape(9, KT, 128).transpose(2, 0, 1))
    g4 = np.concatenate([inp["norm_g"][0], inp["final_g"][None]], 0)
    gT = np.ascontiguousarray(g4.reshape(4, KT, 128).transpose(2, 0, 1))
    w_in = inp["w_in"][0]
    w_pa, w_pb, w_out = inp["w_pa"][0], inp["w_pb"][0], inp["w_out"][0]
    maps = []
    for i in range(NCORES):
        b, t0 = i // 4, (i % 4) * TX
        m = {}
        m["x_tok"] = np.ascontiguousarray(np.concatenate([x[b, t0:t0 + TX], ctxf[i * TC:(i + 1) * TC]], 0))
        m["scin"] = np.ascontiguousarray(np.stack([c[b].reshape(KT, 128).T, c_ctx.reshape(KT, 128).T], -1))
        m["w_ada"] = w_ada; m["bT"] = bT; m["gT"] = gT
        m["wg0"] = inp["ffn_w_gate"][0, 0]; m["wu0"] = inp["ffn_w_up"][0, 0]; m["wd0"] = inp["ffn_w_down"][0, 0]
        m["wg1"] = inp["ffn_w_gate"][0, 1]; m["wu1"] = inp["ffn_w_up"][0, 1]; m["wd1"] = inp["ffn_w_down"][0, 1]
        cols = [128 * i, 1024 + 128 * i, 2048 + 128 * i, 3072 + 128 * i]
        m["w_mix"] = np.ascontiguousarray(np.stack([w_in[:, c0:c0 + 128] for c0 in cols], 1))
        m["w_gt"] = np.ascontiguousarray(np.stack([w_in[:, 4096 + 256 * i:4096 + 256 * i + 256], w_in[:, 6144 + 256 * i:6144 + 256 * i + 256]], 1))
        m["w_pa_s"] = np.ascontiguousarray(np.stack([w_pa[:, 256 * i:256 * i + 256], w_pa[:, 2048 + 256 * i:2048 + 256 * i + 256]], 1))
        m["w_pb_s"] = np.ascontiguousarray(w_pb[:, 256 * i:256 * i + 256])
        m["w_out_s"] = np.ascontiguousarray(w_out[256 * i:256 * i + 256, :])
        m.update(s5_host_inputs(inp, i))
        m.update(hy_host_inputs(inp, i))
        m.update(hy_const_inputs(i))
        sw = inp["hy_short_w"][0].reshape(3, 3, 1024)[:, :, 128 * i:128 * i + 128]
        m["scw"] = np.ascontiguousarray(sw.transpose(2, 1, 0)).astype(f32)
        m["scb"] = np.ascontiguousarray(inp["hy_short_b"][0].reshape(3, 1024)[:, 128 * i:128 * i + 128].T).astype(f32)
        maps.append(m)
    return maps


_NC = {}


def kernel(**inputs):
    inp = {k: np.asarray(v) for k, v in inputs.items()}
    if "nc" not in _NC:
        _NC["nc"] = build_fused()
    maps = host_maps(inp)
    res = run_bass_kernel_spmd(_NC["nc"], maps, core_ids=list(range(NCORES)))
    out = np.zeros((2, 4096, D), np.float32)
    for i in range(NCORES):
        b, t0 = i // 4, (i % 4) * TX
        out[b, t0:t0 + TX] = res.results[i]["out"]
    return out


def build_L2():
    nc = bass.Bass("TRN2", target_bir_lowering=False)
    xT_in = nc.dram_tensor("xT_in", [128, KT, T], F32, kind="ExternalInput")
    modT_in = nc.dram_tensor("modT_in", [128, 9, KT, 2], F32, kind="ExternalInput")
    gT_d = nc.dram_tensor("gT", [128, 4, KT], F32, kind="ExternalInput")
    mo_d = nc.dram_tensor("mo", [128, KT, TX], BF16, kind="ExternalInput")
    w_out = nc.dram_tensor("w_out", [D, D], F32, kind="ExternalInput")
    wg1 = nc.dram_tensor("wg1", [D, DFF], F32, kind="ExternalInput")
    wu1 = nc.dram_tensor("wu1", [D, DFF], F32, kind="ExternalInput")
    wd1 = nc.dram_tensor("wd1", [DFF, D], F32, kind="ExternalInput")
    ident_d = nc.dram_tensor("ident", [128, 128], F32, kind="ExternalInput")
    out_d = nc.dram_tensor("out", [TX, D], F32, kind="ExternalOutput")
    with ExitStack() as ctx:
        C = Ctx(nc, ctx)
        P = C.P
        C.wcnt = 0
        alloc_persist(C)
        modT = C.sb("modT", [128, 9, KT, 2], persist=True)
        gT = C.sb("gTs", [128, 4, KT], persist=True)
        C.phase()
        alloc_common(C)
        allx = [("xT", kt) for kt in range(KT)]
        allh = [("hT", kt) for kt in range(KT)]
        P.dma("sp", C.ident[:, :], ident_d[:, :], writes=["ident"])
        P.dma("sp", modT[:, :, :, :], modT_in[:, :, :, :], writes=["modT"])
        P.dma("sp", gT[:, :, :], gT_d[:, :, :], writes=["gT"])
        P.dma("sp", C.xT[:, :, :], xT_in[:, :, :], writes=allx)
        mo = mkap(C.hT, 0, [list(C.hT[:, :, :].ap[0]), [TX, KT], [1, TX]])
        P.dma("sp", mo, mo_d[:, :, :], writes=allh)
        wo_t = C.sb("wo_t", [128, KT, 128], BF16)
        wo_v = w_out.ap().rearrange("(ft p) d -> p ft d", p=128)
        for d_ in range(KT):
            load_cast(C, wo_v[:, :, d_ * 128:(d_ + 1) * 128], wo_t, KT, 128, "wo_t")
            for (a, b) in CH[:2]:
                bi, bk = C.bank()
                P.op("pe", [lambda e, ft=ft, bk=bk, a=a, b=b: e.matmul(bk[:, 0:512], lhsT=wo_t[:, ft, :], rhs=mo[:, ft, a:b], start=(ft == 0), stop=(ft == KT - 1)) for ft in range(KT)],
                     reads=["wo_t"] + allh, writes=[("bank", bi)])
                P.op("dve", lambda e, bk=bk, a=a, b=b, d_=d_: e.scalar_tensor_tensor(out=C.xT[:, d_, a:b], in0=bk[:, 0:512], scalar=modT[:, 5, d_, 0:1], in1=C.xT[:, d_, a:b], op0=ALU.mult, op1=ALU.add),
                     reads=[("bank", bi), "modT"], writes=[("xT", d_)])
        gp2, sh2, hg2 = emit_mods(C, modT, gT, 2, "s2")
        emit_norm_mod(C, C.xT, C.hT, None, C.rstd, C.onesb, gp2, sh2, "s2")
        emit_ffn(C, wg1, wu1, wd1, C.hT, C.xT, hg2, "s2")
        gpf = C.sb("gpf", [128, KT, 2])
        shf = C.sb("shf", [128, KT, 2])
        for r_ in range(2):
            P.op("dve", lambda e, r_=r_: e.tensor_copy(out=gpf[:, :, r_], in_=gT[:, 3, :]), reads=["gT"], writes=["sfgp"])
        P.op("dve", lambda e: e.memset(shf[:, :, :], 0.0), writes=["sfsh"])
        yT = C.xT
        emit_norm_mod(C, C.xT, C.hT, None, C.rstd, C.onesb, gpf, shf, "sf", out32=yT, okey="xT")
        ost = [C.sb("ost%d" % i, [128, D]) for i in range(2)]
        for tt in range(TX // 128):
            i = tt % 2
            for k4 in range(4):
                bi, bk = C.bank()
                P.op("pe", [lambda e, j=j, k4=k4, bk=bk, tt=tt: e.transpose(out=bk[:, j * 128:(j + 1) * 128], in_=yT[:, k4 * 4 + j, tt * 128:(tt + 1) * 128], identity=C.ident[:, :]) for j in range(4)],
                     reads=[("xT", k4 * 4 + j) for j in range(4)] + ["ident"], writes=[("bank", bi)])
                if k4 % 2 == 0:
                    P.op("act", lambda e, bk=bk, k4=k4, i=i: e.copy(out=ost[i][:, k4 * 512:(k4 + 1) * 512], in_=bk[:, 0:512]), reads=[("bank", bi)], writes=[("ost", i)])
                else:
                    P.op("dve", lambda e, bk=bk, k4=k4, i=i: e.tensor_copy(out=ost[i][:, k4 * 512:(k4 + 1) * 512], in_=bk[:, 0:512]), reads=[("bank", bi)], writes=[("ost", i)])
            P.dma("sp", out_d[tt * 128:(tt + 1) * 128, :], ost[i][:, :], reads=[("ost", i)], writes=["out"])
        P.wait_all("sp", ["out"])
        P.replay()
        print("L2 arena peak words", C.peak, "of", ARENA_WORDS)
    return nc


def kernel(**inputs):
    inp = {k: np.asarray(v) for k, v in inputs.items()}
    if "la" not in _NC:
        _NC["la"] = build_fused(mode="LA")
        _NC["lb"] = build_fused(mode="LB")
        _NC["lm"] = build_fused(mode="LM")
        _NC["l2"] = build_L2()
    maps = host_maps(inp)
    cores = list(range(NCORES))
    ra = run_bass_kernel_spmd(_NC["la"], [{k: m[k] for k in MODE_INPUTS["LA"]} for m in maps], core_ids=cores)
    ago_in = np.concatenate([np.asarray(ra.results[j]["agi"]) for j in cores], 0)
    mb = []
    for m in maps:
        d = {k: m[k] for k in MODE_INPUTS["LB"]}
        d["ago"] = ago_in
        mb.append(d)
    rb = run_bass_kernel_spmd(_NC["lb"], mb, core_ids=cores)
    ag2o_in = np.concatenate([np.asarray(rb.results[j]["ag2i"]) for j in cores], 0)
    mm = []
    for m in maps:
        d = {k: m[k] for k in MODE_INPUTS["LM"]}
        d["ago"] = ago_in
        d["ag2o"] = ag2o_in
        mm.append(d)
    rm = run_bass_kernel_spmd(_NC["lm"], mm, core_ids=cores)
    mfull = np.concatenate([np.asarray(rm.results[j]["mT_o"]).reshape(256, 2 * HL) for j in cores], 0)
    maps2 = []
    for i in cores:
        mo = mfull[:, i * TX:(i + 1) * TX].reshape(KT, 128, TX).transpose(1, 0, 2)
        maps2.append({"xT_in": ra.results[i]["xT_o"], "modT_in": ra.results[i]["modT_o"], "gT": maps[i]["gT"],
                      "mo": np.ascontiguousarray(mo), "w_out": np.ascontiguousarray(inp["w_out"][0]),
                      "wg1": maps[i]["wg1"], "wu1": maps[i]["wu1"], "wd1": maps[i]["wd1"], "ident": maps[i]["ident"]})
    res2 = run_bass_kernel_spmd(_NC["l2"], maps2, core_ids=cores)
    out = np.zeros((2, 4096, D), np.float32)
    for i in cores:
        b, t0 = i // 4, (i % 4) * TX
        out[b, t0:t0 + TX] = res2.results[i]["out"]
    return out
```
